# Optimizing a Trainium2 kernel written in Bass

```python
import jax, jax.numpy as jnp
from jax import lax
import numpy as np

D_MODEL = 1024
BATCH = 8
SEQ = 4096
DEPTH = 2

RET_HEADS = 4
RET_HEAD_DIM = D_MODEL // RET_HEADS
RET_WIDTH = RET_HEADS * RET_HEAD_DIM
RET_CHUNK = 128
ROPE_BASE = 10000.0

CONV_CH = D_MODEL
CONV_K = 3

SWA_HEAD_DIM = 64
SWA_Q_HEADS = D_MODEL // SWA_HEAD_DIM
SWA_KV_HEADS = SWA_Q_HEADS // 8
SWA_WIDTH = SWA_Q_HEADS * SWA_HEAD_DIM
SWA_WINDOW = 128
SWA_BLOCK = 128

N_BRANCH = 3
D_FF = 2816
MACARON_WEIGHT = 0.5
N_MOD = 9
EPS = 1e-6

RET_COLS = 4 * RET_WIDTH
CONV_COLS = 3 * CONV_CH
SWA_COLS = (SWA_Q_HEADS + 2 * SWA_KV_HEADS) * SWA_HEAD_DIM
GATE_COLS = N_BRANCH * D_MODEL
OFF_CONV = RET_COLS
OFF_SWA = OFF_CONV + CONV_COLS
OFF_GATE = OFF_SWA + SWA_COLS
IN_COLS = OFF_GATE + GATE_COLS

kernel_name = 'hybrid_retention_shortconv_swa_macaron'


def rmsnorm(t, g):
    tf = t.astype(jnp.float32)
    n = tf * lax.rsqrt(jnp.mean(tf * tf, axis=-1, keepdims=True) + EPS)
    return n.astype(t.dtype) * g


def modulate(t, shift, scale):
    return t * (1 + scale[:, None, :]) + shift[:, None, :]


def swiglu(t, w1, w3, w2):
    return (jax.nn.silu(t @ w1) * (t @ w3)) @ w2


def rotary(t):
    S, d = t.shape[1], t.shape[-1]
    half = d // 2
    inv = jnp.power(ROPE_BASE, -jnp.linspace(0.0, 1.0, half, dtype=jnp.float32))
    ang = jnp.arange(S, dtype=jnp.float32)[:, None] * inv[None, :]
    cos = jnp.cos(ang)[None, :, None, :]
    sin = jnp.sin(ang)[None, :, None, :]
    t1, t2 = t[..., :half], t[..., half:]
    return jnp.concatenate([t1 * cos - t2 * sin, t2 * cos + t1 * sin], axis=-1)


def retention(q, k, v):
    B, S, H, dk = q.shape
    dv = v.shape[-1]
    C = RET_CHUNK
    N = S // C
    log_gamma = jnp.log1p(-jnp.exp2(-5.0 - jnp.arange(H, dtype=jnp.float32)))
    idx = jnp.arange(C, dtype=jnp.float32)
    rel = idx[:, None] - idx[None, :]
    intra = jnp.where(rel >= 0, jnp.exp(log_gamma[:, None, None] * jnp.maximum(rel, 0.0)), 0.0)
    xi = jnp.exp(log_gamma[:, None] * (idx + 1.0))[None, :, :, None]
    zeta = jnp.exp(log_gamma[:, None] * (C - 1.0 - idx))[None, :, :, None]
    chunk_decay = jnp.exp(log_gamma * C)[None, :, None, None]
    k = k * (dk ** -0.5)

    def to_chunks(t):
        return t.reshape(B, N, C, H, t.shape[-1]).transpose(1, 0, 3, 2, 4)

    def step(state, inp):
        qc, kc, vc = inp
        scores = jnp.einsum('bhid,bhjd->bhij', qc, kc) * intra[None]
        inner = jnp.einsum('bhij,bhje->bhie', scores, vc)
        cross = jnp.einsum('bhid,bhde->bhie', qc, state) * xi
        new_state = state * chunk_decay + jnp.einsum('bhjd,bhje->bhde', kc * zeta, vc)
        return new_state, inner + cross

    state0 = jnp.zeros((B, H, dk, dv), jnp.float32)
    _, out = lax.scan(step, state0, (to_chunks(q), to_chunks(k), to_chunks(v)))
    return out.transpose(1, 0, 3, 2, 4).reshape(B, S, H, dv)


def short_conv(z, w):
    S = z.shape[1]
    zp = jnp.pad(z, ((0, 0), (CONV_K - 1, 0), (0, 0)))
    y = zp[:, 0:S, :] * w[0]
    for j in range(1, CONV_K):
        y = y + zp[:, j:j + S, :] * w[j]
    return y


def swa_with_sinks(q, k, v, sinks):
    B, S, Hq, d = q.shape
    Hkv = k.shape[2]
    G = Hq // Hkv
    W = SWA_BLOCK
    N = S // W
    qb = q.reshape(B, N, W, Hkv, G, d)

    def band(t):
        tp = jnp.pad(t, ((0, 0), (W, 0), (0, 0), (0, 0)))
        prev = tp[:, :S].reshape(B, N, W, Hkv, d)
        cur = t.reshape(B, N, W, Hkv, d)
        return jnp.concatenate([prev, cur], axis=2)

    kb, vb = band(k), band(v)
    scores = jnp.einsum('bnqkgd,bnjkd->bnkgqj', qb, kb).astype(jnp.float32) * (d ** -0.5)
    i = jnp.arange(W)[:, None]
    j = jnp.arange(2 * W)[None, :]
    diff = i + W - j
    in_band = (diff >= 0) & (diff < SWA_WINDOW)
    valid = in_band[None] & ((jnp.arange(N)[:, None, None] > 0) | (j >= W)[None])
    scores = jnp.where(valid[None, :, None, None], scores, -jnp.inf)
    s = sinks.astype(jnp.float32).reshape(Hkv, G)[None, None, :, :, None, None]
    m = jnp.maximum(jnp.max(scores, axis=-1, keepdims=True), s)
    p = jnp.exp(scores - m)
    probs = p / (jnp.sum(p, axis=-1, keepdims=True) + jnp.exp(s - m))
    out = jnp.einsum('bnkgqj,bnjkd->bnqkgd', probs.astype(v.dtype), vb)
    return out.reshape(B, S, Hq * d)


def hybrid_mixer(u, w_in, conv_w, sinks, p_ret, p_conv, p_swa, w_out):
    B, S, _ = u.shape
    proj = u @ w_in
    rq = proj[..., 0:RET_WIDTH].reshape(B, S, RET_HEADS, RET_HEAD_DIM).astype(jnp.float32)
    rk = proj[..., RET_WIDTH:2 * RET_WIDTH].reshape(B, S, RET_HEADS, RET_HEAD_DIM).astype(jnp.float32)
    rv = proj[..., 2 * RET_WIDTH:3 * RET_WIDTH].reshape(B, S, RET_HEADS, RET_HEAD_DIM).astype(jnp.float32)
    rg = proj[..., 3 * RET_WIDTH:RET_COLS]
    r = retention(rotary(rq), rotary(rk), rv)
    r = r * lax.rsqrt(jnp.mean(r * r, axis=-1, keepdims=True) + EPS)
    y_a = jax.nn.silu(rg) * r.reshape(B, S, RET_WIDTH).astype(u.dtype)
    cb = proj[..., OFF_CONV:OFF_CONV + CONV_CH]
    cc = proj[..., OFF_CONV + CONV_CH:OFF_CONV + 2 * CONV_CH]
    cx = proj[..., OFF_CONV + 2 * CONV_CH:OFF_SWA]
    y_b = cb * short_conv(cc * cx, conv_w)
    kv_w = SWA_KV_HEADS * SWA_HEAD_DIM
    sq = proj[..., OFF_SWA:OFF_SWA + SWA_WIDTH].reshape(B, S, SWA_Q_HEADS, SWA_HEAD_DIM)
    sk = proj[..., OFF_SWA + SWA_WIDTH:OFF_SWA + SWA_WIDTH + kv_w].reshape(B, S, SWA_KV_HEADS, SWA_HEAD_DIM)
    sv = proj[..., OFF_SWA + SWA_WIDTH + kv_w:OFF_GATE].reshape(B, S, SWA_KV_HEADS, SWA_HEAD_DIM)
    y_c = swa_with_sinks(sq, sk, sv, sinks)
    gates = jax.nn.sigmoid(proj[..., OFF_GATE:IN_COLS]).reshape(B, S, N_BRANCH, D_MODEL)
    merged = (gates[:, :, 0] * (y_a @ p_ret)
              + gates[:, :, 1] * (y_b @ p_conv)
              + gates[:, :, 2] * (y_c @ p_swa))
    return merged @ w_out


def setup_inputs(seed: int = 0) -> dict:
    key = jax.random.key(seed)
    ks = jax.random.split(key, 24)
    f32 = jnp.float32

    def nrm(k, shape, scale):
        return jax.random.normal(k, shape, f32) * scale

    def gain(k, shape):
        return 1.0 + 0.05 * jax.random.normal(k, shape, f32)

    Dm = D_MODEL
    return {
        'x': nrm(ks[0], (BATCH, SEQ, Dm), 1.0),
        'c': nrm(ks[1], (BATCH, Dm), 1.0),
        'ada_w': nrm(ks[2], (DEPTH, Dm, N_MOD * Dm), 0.5 * Dm ** -0.5),
        'ada_b': nrm(ks[3], (DEPTH, N_MOD * Dm), 0.02),
        'ffn1_norm': gain(ks[4], (DEPTH, Dm)),
        'ffn1_w1': nrm(ks[5], (DEPTH, Dm, D_FF), Dm ** -0.5),
        'ffn1_w3': nrm(ks[6], (DEPTH, Dm, D_FF), Dm ** -0.5),
        'ffn1_w2': nrm(ks[7], (DEPTH, D_FF, Dm), D_FF ** -0.5),
        'mix_norm': gain(ks[8], (DEPTH, Dm)),
        'w_in': nrm(ks[9], (DEPTH, Dm, IN_COLS), Dm ** -0.5),
        'conv_w': nrm(ks[10], (DEPTH, CONV_K, CONV_CH), CONV_K ** -0.5),
        'swa_sinks': nrm(ks[11], (DEPTH, SWA_Q_HEADS), 1.0),
        'p_ret': nrm(ks[12], (DEPTH, RET_WIDTH, Dm), RET_WIDTH ** -0.5),
        'p_conv': nrm(ks[13], (DEPTH, CONV_CH, Dm), CONV_CH ** -0.5),
        'p_swa': nrm(ks[14], (DEPTH, SWA_WIDTH, Dm), SWA_WIDTH ** -0.5),
        'w_out': nrm(ks[15], (DEPTH, Dm, Dm), Dm ** -0.5),
        'ffn2_norm': gain(ks[16], (DEPTH, Dm)),
        'ffn2_w1': nrm(ks[17], (DEPTH, Dm, D_FF), Dm ** -0.5),
        'ffn2_w3': nrm(ks[18], (DEPTH, Dm, D_FF), Dm ** -0.5),
        'ffn2_w2': nrm(ks[19], (DEPTH, D_FF, Dm), D_FF ** -0.5),
        'final_norm': gain(ks[20], (Dm,)),
    }


def reference(x, c, ada_w, ada_b, ffn1_norm, ffn1_w1, ffn1_w3, ffn1_w2, mix_norm, w_in,
              conv_w, swa_sinks, p_ret, p_conv, p_swa, w_out, ffn2_norm, ffn2_w1, ffn2_w3,
              ffn2_w2, final_norm):
    h = x
    c_act = jax.nn.silu(c)
    for l in range(DEPTH):
        mod = c_act @ ada_w[l] + ada_b[l]
        sh1, sc1, g1, sh2, sc2, g2, sh3, sc3, g3 = jnp.split(mod, N_MOD, axis=-1)
        u = modulate(rmsnorm(h, ffn1_norm[l]), sh1, sc1)
        h = h + MACARON_WEIGHT * g1[:, None, :] * swiglu(u, ffn1_w1[l], ffn1_w3[l], ffn1_w2[l])
        u = modulate(rmsnorm(h, mix_norm[l]), sh2, sc2)
        h = h + g2[:, None, :] * hybrid_mixer(u, w_in[l], conv_w[l], swa_sinks[l], p_ret[l],
                                              p_conv[l], p_swa[l], w_out[l])
        u = modulate(rmsnorm(h, ffn2_norm[l]), sh3, sc3)
        h = h + MACARON_WEIGHT * g3[:, None, :] * swiglu(u, ffn2_w1[l], ffn2_w3[l], ffn2_w2[l])
    return rmsnorm(h, final_norm)
```

```python
import contextlib
import numpy as np
import concourse.bass as bass
import concourse.mybir as mybir
from concourse.bass_utils import run_bass_kernel_spmd

F32 = mybir.dt.float32
BF16 = mybir.dt.bfloat16
AF = mybir.ActivationFunctionType
ALU = mybir.AluOpType
AX = mybir.AxisListType

D = 1024
SEQ = 4096
NCORE = 8
DEPTH = 2
TT = 512
DFF = 2816
NBLK = DFF // 128
RET_H = 4
EPS = 1e-6
IN_COLS = 11520
OFF_CONV = 4096
OFF_SWA = OFF_CONV + 3072
OFF_GATE = OFF_SWA + 1280
NW = 5
NPS = 6
NTF = 10
NTB = 8

U_ADA = 36
U_F13 = 22
U_F2 = 16
U_IN = 45
U_P = 16


class Sched:
    ENGS = ("pe", "act", "dve", "pool", "sp")

    def __init__(self, nc):
        self.nc = nc
        self.ops = {e: [] for e in self.ENGS}
        self.last_w = {}
        self.readers = {}
        self.chan_n = {}
        self.chan_last = {}

    def _collect(self, reads, writes):
        deps = {}
        for r in reads:
            t = self.last_w.get(r)
            if t is not None:
                deps[t] = True
        for w in writes:
            t = self.last_w.get(w)
            if t is not None and t not in deps:
                deps[t] = False
            for t in self.readers.get(w, {}).values():
                if t not in deps:
                    deps[t] = False
        return deps

    def _commit(self, token, key, reads, writes):
        for r in reads:
            self.readers.setdefault(r, {})[key] = token
        for w in writes:
            self.last_w[w] = token
            self.readers[w] = {}

    def op(self, eng, fn, reads=(), writes=()):
        deps = self._collect(reads, writes)
        idx = len(self.ops[eng])
        token = ("c", eng, idx)
        self.ops[eng].append(dict(fn=fn, deps=deps, kind="c"))
        self._commit(token, eng, reads, writes)
        return token

    def dma(self, q, fn, reads=(), writes=(), chan=None):
        deps = self._collect(reads, writes)
        prev = self.chan_last.get(chan)
        if prev is not None:
            deps.setdefault(prev, True)
        k = self.chan_n.get(chan, 0) + 1
        self.chan_n[chan] = k
        token = ("d", chan, k)
        self.chan_last[chan] = token
        self.ops[q].append(dict(fn=fn, deps=deps, kind="d", chan=chan))
        self._commit(token, ("d", chan), reads, writes)
        return token

    def emit(self, final_tokens=()):
        nc = self.nc
        need = {e: set() for e in self.ENGS}
        for e in self.ENGS:
            for o in self.ops[e]:
                for t, raw in o["deps"].items():
                    if t[0] == "c":
                        if t[1] == e and (e == "pe" or not raw):
                            continue
                        need[t[1]].add(t[2])
        for t in final_tokens:
            if t[0] == "c":
                need[t[1]].add(t[2])
        vals = {}
        for e in self.ENGS:
            c = 0
            for i, o in enumerate(self.ops[e]):
                if o["kind"] == "c" and i in need[e]:
                    c += 1
                    vals[(e, i)] = c
        es = contextlib.ExitStack()
        sem_e = {e: es.enter_context(nc.semaphore("s_" + e)) for e in ("pe", "act", "dve", "pool")}
        sem_c = {ch: es.enter_context(nc.semaphore("d_%d" % i)) for i, ch in enumerate(self.chan_n)}
        handles = {"pe": nc.tensor, "act": nc.scalar, "dve": nc.vector, "pool": nc.gpsimd, "sp": nc.sync}
        self.nwaits = 0

        def run_stream(e, extra_final=False):
            h = handles[e]
            known = {}
            for i, o in enumerate(self.ops[e]):
                for t, raw in o["deps"].items():
                    if t[0] == "c":
                        if t[1] == e and (e == "pe" or not raw):
                            continue
                        sem, v = sem_e[t[1]], vals[(t[1], t[2])]
                        kk = ("c", t[1])
                    else:
                        sem, v = sem_c[t[1]], 16 * t[2]
                        kk = ("d", t[1])
                    if known.get(kk, 0) >= v:
                        continue
                    known[kk] = v
                    h.wait_ge(sem, v)
                    self.nwaits += 1
                ins = o["fn"]()
                if o["kind"] == "d":
                    ins.then_inc(sem_c[o["chan"]], 16)
                elif (e, i) in vals:
                    ins.then_inc(sem_e[e], 1)
            if extra_final:
                for t in final_tokens:
                    if t[0] == "c":
                        h.wait_ge(sem_e[t[1]], vals[(t[1], t[2])])
                    else:
                        h.wait_ge(sem_c[t[1]], 16 * t[2])

        with nc.Block() as block:
            @block.sync
            def _(x):
                run_stream("sp", extra_final=True)

            @block.tensor
            def _(x):
                run_stream("pe")

            @block.scalar
            def _(x):
                run_stream("act")

            @block.vector
            def _(x):
                run_stream("dve")

            @block.gpsimd
            def _(x):
                run_stream("pool")
        es.close()


def _pack_stationary(W, col_chunks, rowperm=None):
    if rowperm is not None:
        W = W[rowperm]
    out = np.empty((len(col_chunks), 128, 8, 128), np.float32)
    Wr = W.reshape(8, 128, W.shape[1])
    for c, cols in enumerate(col_chunks):
        if isinstance(cols, slice):
            blk = Wr[:, :, cols]
        else:
            blk = Wr[:, :, cols]
        out[c] = blk.transpose(1, 0, 2)
    return out


def _pair_units(chunks):
    n2 = chunks.shape[0]
    assert n2 % 2 == 0
    u = chunks.reshape(n2 // 2, 2, 128, 8, 128).transpose(0, 2, 1, 3, 4)
    return u.reshape(n2 // 2 * 128, 2048)


def _sl(a, n=128):
    return slice(a, a + n)


def _swa_q_cols(c):
    return np.concatenate([OFF_SWA + c * 64 + np.arange(64), OFF_SWA + (c + 8) * 64 + np.arange(64)])


_SWA_ROWPERM = np.array([(kc + 8 * (p // 64)) * 64 + p % 64 for kc in range(8) for p in range(128)])


def _layer_units_in(w_in):
    units = []
    for hd in range(RET_H):
        base = hd * 256
        for off in (0, 1024, 3072):
            ch = _pack_stationary(w_in, [_sl(off + base), _sl(off + base + 128)])
            units.append(_pair_units(ch))
        v = w_in[:, 2048 + base:2048 + base + 256].reshape(8, 128, 256).transpose(1, 0, 2)
        units.append(np.ascontiguousarray(v).reshape(128, 2048))
    for c in range(8):
        ch = _pack_stationary(w_in, [_sl(OFF_CONV + 1024 + c * 128), _sl(OFF_CONV + 2048 + c * 128)])
        units.append(_pair_units(ch))
    for c2 in range(4):
        ch = _pack_stationary(w_in, [_sl(OFF_CONV + (2 * c2) * 128), _sl(OFF_CONV + (2 * c2 + 1) * 128)])
        units.append(_pair_units(ch))
    for c2 in range(4):
        ch = _pack_stationary(w_in, [_swa_q_cols(2 * c2), _swa_q_cols(2 * c2 + 1)])
        units.append(_pair_units(ch))
    ch = _pack_stationary(w_in, [_sl(OFF_SWA + 1024), _sl(OFF_SWA + 1024 + 128)])
    units.append(_pair_units(ch))
    for b in range(3):
        for c2 in range(4):
            ch = _pack_stationary(w_in, [_sl(OFF_GATE + b * 1024 + (2 * c2) * 128), _sl(OFF_GATE + b * 1024 + (2 * c2 + 1) * 128)])
            units.append(_pair_units(ch))
    u = np.concatenate(units, axis=0)
    assert u.shape == (U_IN * 128, 2048)
    return u


def _sq_units(W, rowperm=None):
    ch = _pack_stationary(W, [_sl(c * 128) for c in range(8)], rowperm)
    return _pair_units(ch)


def _ffn13_units(w1, w3):
    a = w1.reshape(8, 128, NBLK, 128).transpose(2, 1, 0, 3)
    b = w3.reshape(8, 128, NBLK, 128).transpose(2, 1, 0, 3)
    u = np.stack([a, b], axis=2)
    return np.ascontiguousarray(u).reshape(NBLK * 128, 2048)


def _ffn2_units(w2):
    a = w2.reshape(NBLK, 128, 8, 128).transpose(2, 1, 0, 3)
    a = a.reshape(8, 128, 2, 11, 128).transpose(0, 2, 1, 3, 4)
    return np.ascontiguousarray(a).reshape(16 * 128, 1408)


def _ada_units(ada_w):
    ch = ada_w.reshape(8, 128, 72, 128).transpose(2, 1, 0, 3)
    return _pair_units(np.ascontiguousarray(ch))


def _fm(v):
    return np.ascontiguousarray(v.reshape(-1, 128).T)


def _const_tables(seq):
    h = np.arange(RET_H, dtype=np.float64)
    gam = 1.0 - np.exp2(-5.0 - h)
    i = np.arange(128, dtype=np.float64)
    rel = i[None, :] - i[:, None]
    intraT = np.where(rel[None] >= 0, gam[:, None, None] ** np.maximum(rel[None], 0.0), 0.0)
    xi = gam[None, :] ** (i[:, None] + 1.0)
    zeta = gam[None, :] ** (127.0 - i[:, None])
    cd = [float(np.float32(g ** 128)) for g in gam]
    cF = np.zeros((128, 128 + 512 + 4 + 4 + 1), np.float32)
    cF[:, 0:128] = 1.0
    cF[:, 128:640] = intraT.transpose(1, 0, 2).reshape(128, 512)
    cF[:, 640:644] = xi
    cF[:, 644:648] = zeta
    cF[:, 648] = EPS
    cB = np.zeros((128, 128 + 64 + 1024), np.float32)
    cB[:, 0:128] = np.eye(128)
    cB[:, 128:192] = 1.0
    j = np.arange(128)[:, None]
    ii = np.arange(128)[None, :]
    mprev = (j > ii).astype(np.float32)
    mcur = (j <= ii).astype(np.float32)
    cB[:, 192:192 + 512] = np.tile(mprev, (1, 4))
    cB[:, 192 + 512:192 + 1024] = np.tile(mcur, (1, 4))
    inv = np.power(np.float32(10000.0), -np.linspace(0.0, 1.0, 128, dtype=np.float32)).astype(np.float32)
    ang = (np.arange(seq, dtype=np.float32)[:, None] * inv[None, :]).astype(np.float32)
    rope = np.stack([np.cos(ang.astype(np.float64)).T, np.sin(ang.astype(np.float64)).T], axis=1).astype(np.float32)
    return cF, cB, np.ascontiguousarray(rope), cd


def build_program(seq=SEQ, depth=DEPTH, cfg=None):
    cfg = cfg or {}
    do_ffn1 = cfg.get("ffn1", True)
    do_mix = cfg.get("mix", True)
    do_ffn2 = cfg.get("ffn2", True)
    do_ret = cfg.get("ret", True)
    do_conv = cfg.get("conv", True)
    do_swa = cfg.get("swa", True)
    ntiles = seq // TT
    L = depth
    cd = _const_tables(128)[3]

    nc = bass.Bass("TRN2", target_bir_lowering=False)
    xT = nc.dram_tensor("xT", [128, 8, seq], F32, kind="ExternalInput").ap()
    cTd = nc.dram_tensor("cT", [128, 8], F32, kind="ExternalInput").ap()
    vecsd = nc.dram_tensor("vecs", [128, L * 128 + 8], F32, kind="ExternalInput").ap()
    cFd = nc.dram_tensor("cF", [128, 649], F32, kind="ExternalInput").ap()
    cBd = nc.dram_tensor("cB", [128, 1216], F32, kind="ExternalInput").ap()
    roped = nc.dram_tensor("rope", [128, 2, seq], F32, kind="ExternalInput").ap()
    w_ada = nc.dram_tensor("w_ada", [L * U_ADA * 128, 2048], F32, kind="ExternalInput").ap()
    w_f13 = nc.dram_tensor("w_f13", [L * 2 * U_F13 * 128, 2048], F32, kind="ExternalInput").ap()
    w_f2 = nc.dram_tensor("w_f2", [L * 2 * U_F2 * 128, 1408], F32, kind="ExternalInput").ap()
    w_in = nc.dram_tensor("w_in", [L * U_IN * 128, 2048], F32, kind="ExternalInput").ap()
    w_p = nc.dram_tensor("w_p", [L * U_P * 128, 2048], F32, kind="ExternalInput").ap()
    oT = nc.dram_tensor("oT", [128, 8, seq], F32, kind="ExternalOutput").ap()

    S = Sched(nc)
    A = nc.alloc_sbuf_tensor
    hT = [A("hT%d" % i, [128, 8, TT], F32) for i in range(2)]
    uT = A("uT", [128, 8, TT], BF16)
    wb = [A("wb%d" % i, [128, 2048], BF16) for i in range(NW)]
    arena = A("arena", [128, 24, TT], BF16)
    tF = [A("tF%d" % i, [128, TT], F32) for i in range(NTF)]
    tB = [A("tB%d" % i, [128, TT], BF16) for i in range(NTB)]
    ropeS = A("ropeS", [128, 2, TT], F32)
    S32 = A("S32", [128, L * RET_H, 512], F32)
    Sbf = A("Sbf", [128, L * RET_H, 512], BF16)
    merged = A("merged", [128, 8, TT], F32)
    qTr = [A("qTr%d" % i, [128, 2, TT], BF16) for i in range(2)]
    kTr = [A("kTr%d" % i, [128, 2, TT], BF16) for i in range(2)]
    gS = [A("gS%d" % i, [128, 2, TT], BF16) for i in range(1)]
    vtok = [A("vtok%d" % i, [128, 4, 256], BF16) for i in range(1)]
    ktok = [A("ktok%d" % i, [128, 4, 256], BF16) for i in range(1)]
    PTr = [A("PTr%d" % i, [128, 128], BF16) for i in range(2)]
    cxs = [A("cxs%d" % i, [128, 256], F32) for i in range(2)]
    otok = [A("otok%d" % i, [128, 256], F32) for i in range(2)]
    ontok = [A("ontok%d" % i, [128, 256], BF16) for i in range(2)]
    junk = A("junk", [128, 256], F32)
    st = [A("st%d" % i, [128, 4], F32) for i in range(2)]
    zb = [A("zb%d" % i, [128, TT + 2], F32) for i in range(2)]
    convst = A("convst", [128, L * 8, 2], F32)
    skT = [A("skT%d" % l, [128, 128 + TT], BF16) for l in range(L)]
    svt = [A("svt%d" % l, [128, 5, 128], BF16) for l in range(L)]
    cF = A("cFs", [128, 649], F32)
    cB = A("cBs", [128, 1216], BF16)
    vecs = A("vecss", [128, L * 128 + 8], F32)
    cT = A("cTs", [128, 8], F32)
    cact = A("cact", [128, 8], BF16)
    modv = A("modv", [128, L, 72], F32)
    der = A("der", [128, L, 5, 8], F32)
    esb = A("esb", [128, L, 8, 128], F32)
    es = A("es", [128, L, 8], F32)
    ps = [nc.alloc_psum_tensor("ps%d" % i, [128, 512], F32) for i in range(NPS)]
    pt = [nc.alloc_psum_tensor("pt%d" % i, [128, 1024], BF16) for i in range(2)]

    ones_f = cF[:, 0:128]
    eps_ap = cF[:, 648:649]
    ident = cB[:, 0:128]
    ones_b = cB[:, 128:192]

    ctr = dict(ps=0, tf=0, tb=0, wb=0, pt=0)

    def new_ps():
        i = ctr["ps"] % NPS
        ctr["ps"] += 1
        return ps[i], ("ps", i)

    def new_pt():
        i = ctr["pt"] % 2
        ctr["pt"] += 1
        return pt[i], ("pt", i)

    def new_tf():
        i = ctr["tf"] % NTF
        ctr["tf"] += 1
        return tF[i], ("tf", i)

    def new_tb():
        i = ctr["tb"] % NTB
        ctr["tb"] += 1
        return tB[i], ("tb", i)

    def wload(dram, row0, ncols=2048):
        k = ctr["wb"] % NW
        ctr["wb"] += 1
        src = dram[row0:row0 + 128, 0:ncols]
        S.dma("pool", lambda: nc.gpsimd.dma_start(out=wb[k][:, 0:ncols], in_=src), writes=[("wb", k)], chan=("wb", k))
        return wb[k], ("wb", k)

    def mm(out, lhsT, rhs, start, stop, reads, writes):
        S.op("pe", lambda: nc.tensor.matmul(out, lhsT=lhsT, rhs=rhs, start=start, stop=stop), reads=reads, writes=writes)

    def proj_fm(w, wk, o, rhs_of_kc, rkeys, pst, pk, cols=slice(0, TT)):
        for kc in range(8):
            mm(pst[:, cols], w[:, (o * 8 + kc) * 128:(o * 8 + kc + 1) * 128], rhs_of_kc(kc), kc == 0, kc == 7,
               [wk, rkeys[kc]], [pk])

    uT_kc = lambda kc: uT[:, kc, :]
    uT_keys = [("u", kc) for kc in range(8)]

    S.dma("sp", lambda: nc.sync.dma_start(out=cF[:], in_=cFd), writes=["cF"], chan="c0")
    S.dma("sp", lambda: nc.sync.dma_start(out=vecs[:], in_=vecsd), writes=["vecs"], chan="c1")
    S.dma("sp", lambda: nc.sync.dma_start(out=cT[:], in_=cTd), writes=["cT"], chan="c2")
    S.dma("pool", lambda: nc.gpsimd.dma_start(out=cB[:], in_=cBd), writes=["cB"], chan="c3")
    S.op("act", lambda: nc.scalar.activation(out=cact[:], in_=cT[:], func=AF.Silu), reads=["cT"], writes=["cact"])
    S.op("dve", lambda: nc.vector.memset(S32[:], 0.0), writes=[("S32", i) for i in range(L * RET_H)])
    S.op("dve", lambda: nc.vector.memset(Sbf[:], 0.0), writes=[("Sbf", i) for i in range(L * RET_H)])
    S.op("dve", lambda: nc.vector.memset(convst[:], 0.0), writes=[("cst", i) for i in range(L * 8)])
    for l in range(L):
        S.op("dve", lambda l=l: nc.vector.memset(skT[l][:], 0.0), writes=[("skT", l)])
        S.op("dve", lambda l=l: nc.vector.memset(svt[l][:], 0.0), writes=[("svt", l)])
    for l in range(L):
        pm, pmk = new_ps()
        for u in range(U_ADA):
            w, wk = wload(w_ada, (l * U_ADA + u) * 128)
            for o in range(2):
                oc = 2 * u + o
                for kc in range(8):
                    mm(pm[:, oc:oc + 1], w[:, (o * 8 + kc) * 128:(o * 8 + kc + 1) * 128], cact[:, kc:kc + 1],
                       kc == 0, kc == 7, [wk, "cact"], [pmk])
        vb = l * 128
        S.op("dve", lambda l=l, pm=pm, vb=vb: nc.vector.tensor_tensor(out=modv[:, l, :], in0=pm[:, 0:72], in1=vecs[:, vb:vb + 72], op=ALU.add),
             reads=[pmk, "vecs"], writes=[("mod", l)])
        for di, (sc_i, nv) in enumerate(((1, 72), (4, 80), (7, 88))):
            dst = (0, 2, 3)[di]
            S.op("dve", lambda l=l, dst=dst, sc_i=sc_i, nv=nv, vb=vb: nc.vector.scalar_tensor_tensor(
                out=der[:, l, dst, :], in0=modv[:, l, sc_i * 8:(sc_i + 1) * 8], scalar=1.0, in1=vecs[:, vb + nv:vb + nv + 8],
                op0=ALU.add, op1=ALU.mult), reads=[("mod", l), "vecs"], writes=[("der", l)])
        for dst, g_i in ((1, 2), (4, 8)):
            S.op("dve", lambda l=l, dst=dst, g_i=g_i: nc.vector.tensor_scalar(
                out=der[:, l, dst, :], in0=modv[:, l, g_i * 8:(g_i + 1) * 8], scalar1=0.5, scalar2=None, op0=ALU.mult),
                reads=[("mod", l)], writes=[("der", l)])
        S.op("act", lambda l=l, vb=vb: nc.scalar.activation(out=es[:, l, :], in_=vecs[:, vb + 120:vb + 128], func=AF.Exp),
             reads=["vecs"], writes=[("es", l)])
        for c in range(8):
            S.op("dve", lambda l=l, c=c: nc.vector.tensor_scalar(out=esb[:, l, c, :], in0=ones_f, scalar1=es[:, l, c:c + 1], scalar2=None, op0=ALU.mult),
                 reads=[("es", l), "cF"], writes=[("esb", l)])

    def norm_mod(h, hk, a_of, b_of, extra_reads, out_fn):
        pss, pk = new_ps()
        for kc in range(8):
            sq, sk = new_tf()
            S.op("act", lambda kc=kc, sq=sq: nc.scalar.activation(out=sq[:], in_=h[:, kc, :], func=AF.Square),
                 reads=[hk(kc)], writes=[sk])
            mm(pss[:], ones_f, sq[:], kc == 0, kc == 7, [sk, "cF"], [pk])
        std, stk = new_tf()
        S.op("act", lambda: nc.scalar.activation(out=std[:], in_=pss[:], func=AF.Sqrt, scale=1.0 / D, bias=eps_ap),
             reads=[pk, "cF"], writes=[stk])
        rstd, rk = new_tf()
        S.op("dve", lambda: nc.vector.reciprocal(out=rstd[:], in_=std[:]), reads=[stk], writes=[rk])
        for kc in range(8):
            out_fn(kc, rstd, rk)

    def norm_to_u(par, l, a_idx, sh_idx):
        h = hT[par]
        hk = lambda kc: ("h", par, kc)

        def out_fn(kc, rstd, rk):
            nt, nk = new_tf()
            S.op("dve", lambda: nc.vector.scalar_tensor_tensor(out=nt[:], in0=h[:, kc, :], scalar=der[:, l, a_idx, kc:kc + 1], in1=rstd[:],
                                                               op0=ALU.mult, op1=ALU.mult),
                 reads=[hk(kc), rk, ("der", l)], writes=[nk])
            S.op("act", lambda: nc.scalar.activation(out=uT[:, kc, :], in_=nt[:], func=AF.Identity,
                                                      bias=modv[:, l, sh_idx * 8 + kc:sh_idx * 8 + kc + 1], scale=1.0),
                 reads=[nk, ("mod", l)], writes=[("u", kc)])
        norm_mod(h, hk, None, None, None, out_fn)

    def ffn(par, l, which, gs_idx):
        h = hT[par]
        b13 = ((l * 2 + which) * U_F13) * 128
        b2 = ((l * 2 + which) * U_F2) * 128
        for blk in range(NBLK):
            w, wk = wload(w_f13, b13 + blk * 128)
            pA, pAk = new_ps()
            pB, pBk = new_ps()
            proj_fm(w, wk, 0, uT_kc, uT_keys, pA, pAk)
            proj_fm(w, wk, 1, uT_kc, uT_keys, pB, pBk)
            s, sk = new_tf()
            S.op("act", lambda pA=pA, s=s: nc.scalar.activation(out=s[:], in_=pA[:], func=AF.Silu), reads=[pAk], writes=[sk])
            S.op("dve", lambda pB=pB, s=s, blk=blk: nc.vector.tensor_tensor(out=arena[:, blk, :], in0=pB[:], in1=s[:], op=ALU.mult),
                 reads=[pBk, sk], writes=[("ar", blk)])
        for oc in range(8):
            pC, pCk = new_ps()
            for half in range(2):
                w, wk = wload(w_f2, b2 + (oc * 2 + half) * 128, 1408)
                for j in range(11):
                    blk = half * 11 + j
                    mm(pC[:], w[:, j * 128:(j + 1) * 128], arena[:, blk, :], blk == 0, blk == NBLK - 1, [wk, ("ar", blk)], [pCk])
            S.op("dve", lambda pC=pC, oc=oc: nc.vector.scalar_tensor_tensor(out=h[:, oc, :], in0=pC[:], scalar=der[:, l, gs_idx, oc:oc + 1],
                                                                            in1=h[:, oc, :], op0=ALU.mult, op1=ALU.add),
                 reads=[pCk, ("der", l), ("h", par, oc)], writes=[("h", par, oc)])

    def rope_evac(pq, pqk, dst, dstk, kscale):
        cos = ropeS[:, 0, :]
        sin = ropeS[:, 1, :]
        ts = []
        for (src, tab) in ((0, cos), (1, sin), (1, cos), (0, sin)):
            t, tk = new_tf()
            if kscale is None:
                S.op("dve", lambda t=t, src=src, tab=tab: nc.vector.tensor_tensor(out=t[:], in0=pq[src][:], in1=tab, op=ALU.mult),
                     reads=[pqk[src], "rope"], writes=[tk])
            else:
                S.op("dve", lambda t=t, src=src, tab=tab: nc.vector.scalar_tensor_tensor(out=t[:], in0=pq[src][:], scalar=kscale, in1=tab,
                                                                                        op0=ALU.mult, op1=ALU.mult),
                     reads=[pqk[src], "rope"], writes=[tk])
            ts.append((t, tk))
        S.op("dve", lambda: nc.vector.tensor_tensor(out=dst[:, 0, :], in0=ts[0][0][:], in1=ts[1][0][:], op=ALU.subtract),
             reads=[ts[0][1], ts[1][1]], writes=[dstk])
        S.op("dve", lambda: nc.vector.tensor_tensor(out=dst[:, 1, :], in0=ts[2][0][:], in1=ts[3][0][:], op=ALU.add),
             reads=[ts[2][1], ts[3][1]], writes=[dstk])

    def retention(l, t):
        ub = (l * U_IN) * 128
        for hd in range(RET_H):
            sset = hd % 2
            sidx = l * RET_H + hd
            q_d, k_d, g_d = qTr[sset], kTr[sset], gS[0]
            qk_, kk_, gk_ = ("qTr", sset), ("kTr", sset), ("gS", 0)
            vt, vtk = vtok[0], ("vtok", 0)
            kt, ktk = ktok[0], ("ktok", 0)
            w, wk = wload(w_in, ub + (hd * 4 + 0) * 128)
            pq = []
            pqk = []
            for dc in range(2):
                p_, pk_ = new_ps()
                proj_fm(w, wk, dc, uT_kc, uT_keys, p_, pk_)
                pq.append(p_)
                pqk.append(pk_)
            rope_evac(pq, pqk, q_d, qk_, None)
            w, wk = wload(w_in, ub + (hd * 4 + 1) * 128)
            pq = []
            pqk = []
            for dc in range(2):
                p_, pk_ = new_ps()
                proj_fm(w, wk, dc, uT_kc, uT_keys, p_, pk_)
                pq.append(p_)
                pqk.append(pk_)
            rope_evac(pq, pqk, k_d, kk_, 1.0 / 16.0)
            w, wk = wload(w_in, ub + (hd * 4 + 2) * 128)
            for dc in range(2):
                p_, pk_ = new_ps()
                proj_fm(w, wk, dc, uT_kc, uT_keys, p_, pk_)
                S.op("act", lambda p_=p_, dc=dc, g_d=g_d: nc.scalar.activation(out=g_d[:, dc, :], in_=p_[:], func=AF.Silu),
                     reads=[pk_], writes=[gk_])
            w, wk = wload(w_in, ub + (hd * 4 + 3) * 128)
            for c2 in range(2):
                p_, pk_ = new_ps()
                for cc in range(2):
                    ch = 2 * c2 + cc
                    for kc in range(8):
                        mm(p_[:, cc * 256:(cc + 1) * 256], uT[:, kc, ch * 128:(ch + 1) * 128], w[:, kc * 256:(kc + 1) * 256],
                           kc == 0, kc == 7, [wk, ("u", kc)], [pk_])
                S.op("act", lambda p_=p_, c2=c2, vt=vt: nc.scalar.copy(out=vt[:, 2 * c2:2 * c2 + 2, :], in_=p_[:]),
                     reads=[pk_], writes=[vtk])
            pT, pTk = new_pt()
            for ch in range(4):
                for dc in range(2):
                    S.op("pe", lambda ch=ch, dc=dc, pT=pT, k_d=k_d: nc.tensor.transpose(
                        out=pT[:, ch * 256 + dc * 128:ch * 256 + (dc + 1) * 128], in_=k_d[:, dc, ch * 128:(ch + 1) * 128], identity=ident),
                        reads=[kk_, "cB"], writes=[pTk])
            S.op("act", lambda pT=pT, kt=kt, hd=hd: nc.scalar.activation(out=kt[:], in_=pT[:], func=AF.Copy, scale=cF[:, 644 + hd:645 + hd]),
                 reads=[pTk, "cF"], writes=[ktk])
            pY, pYk = new_pt()
            for ch in range(4):
                cs = slice(ch * 128, (ch + 1) * 128)
                b = ch % 2
                pS, pSk = new_ps()
                for dc in range(2):
                    mm(pS[:, 0:128], k_d[:, dc, cs], q_d[:, dc, cs], dc == 0, dc == 1, [kk_, qk_], [pSk])
                S.op("dve", lambda pS=pS, b=b, hd=hd: nc.vector.tensor_tensor(out=PTr[b][:], in0=pS[:, 0:128], in1=cF[:, 128 + hd * 128:256 + hd * 128], op=ALU.mult),
                     reads=[pSk, "cF"], writes=[("PTr", b)])
                pO, pOk = new_ps()
                mm(pO[:, 0:256], PTr[b][:], vt[:, ch, :], True, True, [("PTr", b), vtk], [pOk])
                for dc in range(2):
                    mm(pO[:, 256:512], q_d[:, dc, cs], Sbf[:, sidx, dc * 256:(dc + 1) * 256], dc == 0, dc == 1, [qk_, ("Sbf", sidx)], [pOk])
                S.op("act", lambda pO=pO, b=b, hd=hd: nc.scalar.activation(out=cxs[b][:], in_=pO[:, 256:512], func=AF.Copy, scale=cF[:, 640 + hd:641 + hd]),
                     reads=[pOk, "cF"], writes=[("cxs", b)])
                S.op("dve", lambda pO=pO, b=b: nc.vector.tensor_tensor(out=otok[b][:], in0=pO[:, 0:256], in1=cxs[b][:], op=ALU.add),
                     reads=[pOk, ("cxs", b)], writes=[("otok", b)])
                pU, pUk = new_ps()
                for dc in range(2):
                    mm(pU[:, dc * 256:(dc + 1) * 256], kt[:, ch, dc * 128:(dc + 1) * 128], vt[:, ch, :], True, True, [ktk, vtk], [pUk])
                S.op("dve", lambda pU=pU, sidx=sidx, hd=hd: nc.vector.scalar_tensor_tensor(out=S32[:, sidx, :], in0=S32[:, sidx, :], scalar=cd[hd], in1=pU[:],
                                                                                          op0=ALU.mult, op1=ALU.add),
                     reads=[pUk, ("S32", sidx)], writes=[("S32", sidx)])
                S.op("act", lambda sidx=sidx: nc.scalar.copy(out=Sbf[:, sidx, :], in_=S32[:, sidx, :]), reads=[("S32", sidx)], writes=[("Sbf", sidx)])
                S.op("dve", lambda b=b: nc.vector.tensor_tensor(out=junk[:], in0=otok[b][:], in1=otok[b][:], op=ALU.mult),
                     reads=[("otok", b)], writes=["junk"])
                S.op("dve", lambda b=b: nc.vector.tensor_reduce(out=st[b][:, 0:1], in_=junk[:], axis=AX.X, op=ALU.add),
                     reads=["junk"], writes=[("st0", b)])
                S.op("act", lambda b=b: nc.scalar.activation(out=st[b][:, 1:2], in_=st[b][:, 0:1], func=AF.Sqrt, scale=1.0 / 256, bias=eps_ap),
                     reads=[("st0", b), "cF"], writes=[("st1", b)])
                S.op("dve", lambda b=b: nc.vector.reciprocal(out=st[b][:, 2:3], in_=st[b][:, 1:2]), reads=[("st1", b)], writes=[("st2", b)])
                S.op("act", lambda b=b: nc.scalar.activation(out=ontok[b][:], in_=otok[b][:], func=AF.Copy, scale=st[b][:, 2:3]),
                     reads=[("otok", b), ("st2", b)], writes=[("ontok", b)])
                for ec in range(2):
                    S.op("pe", lambda b=b, ec=ec, pY=pY, ch=ch: nc.tensor.transpose(
                        out=pY[:, ec * 512 + ch * 128:ec * 512 + (ch + 1) * 128], in_=ontok[b][:, ec * 128:(ec + 1) * 128], identity=ident),
                        reads=[("ontok", b), "cB"], writes=[pYk])
            for ec in range(2):
                S.op("dve", lambda ec=ec, pY=pY, g_d=g_d, hd=hd: nc.vector.tensor_tensor(out=arena[:, 2 * hd + ec, :], in0=pY[:, ec * 512:(ec + 1) * 512],
                                                                                      in1=g_d[:, ec, :], op=ALU.mult),
                     reads=[pYk, gk_], writes=[("ar", 2 * hd + ec)])

    def conv_branch(l, t):
        ub = (l * U_IN + 16) * 128
        vb = l * 128 + 96
        wcb = None
        for c in range(8):
            w, wk = wload(w_in, ub + c * 128)
            pcc, pcck = new_ps()
            pcx, pcxk = new_ps()
            proj_fm(w, wk, 0, uT_kc, uT_keys, pcc, pcck)
            proj_fm(w, wk, 1, uT_kc, uT_keys, pcx, pcxk)
            ccs, ccsk = new_tf()
            S.op("act", lambda pcc=pcc, ccs=ccs: nc.scalar.copy(out=ccs[:], in_=pcc[:]), reads=[pcck], writes=[ccsk])
            z = zb[c % 2]
            zk = ("zb", c % 2)
            ck = ("cst", l * 8 + c)
            S.op("act", lambda z=z, c=c: nc.scalar.copy(out=z[:, 0:2], in_=convst[:, l * 8 + c, :]), reads=[ck], writes=[zk])
            S.op("dve", lambda z=z, pcx=pcx, ccs=ccs: nc.vector.tensor_tensor(out=z[:, 2:TT + 2], in0=pcx[:], in1=ccs[:], op=ALU.mult),
                 reads=[pcxk, ccsk], writes=[zk])
            S.op("act", lambda z=z, c=c: nc.scalar.copy(out=convst[:, l * 8 + c, :], in_=z[:, TT:TT + 2]), reads=[zk], writes=[ck])
            a0, a0k = new_tf()
            S.op("dve", lambda z=z, a0=a0, c=c: nc.vector.tensor_scalar(out=a0[:], in0=z[:, 0:TT], scalar1=vecs[:, vb + c:vb + c + 1], scalar2=None, op0=ALU.mult),
                 reads=[zk, "vecs"], writes=[a0k])
            a1, a1k = new_tf()
            S.op("dve", lambda z=z, a0=a0, a1=a1, c=c: nc.vector.scalar_tensor_tensor(out=a1[:], in0=z[:, 1:TT + 1], scalar=vecs[:, vb + 8 + c:vb + 9 + c], in1=a0[:],
                                                                                   op0=ALU.mult, op1=ALU.add),
                 reads=[zk, "vecs", a0k], writes=[a1k])
            a2, a2k = new_tf()
            S.op("dve", lambda z=z, a1=a1, a2=a2, c=c: nc.vector.scalar_tensor_tensor(out=a2[:], in0=z[:, 2:TT + 2], scalar=vecs[:, vb + 16 + c:vb + 17 + c], in1=a1[:],
                                                                                   op0=ALU.mult, op1=ALU.add),
                 reads=[zk, "vecs", a1k], writes=[a2k])
            if c % 2 == 0:
                wcb, wcbk = wload(w_in, (l * U_IN + 24 + c // 2) * 128)
            pcb, pcbk = new_ps()
            proj_fm(wcb, wcbk, c % 2, uT_kc, uT_keys, pcb, pcbk)
            S.op("dve", lambda pcb=pcb, a2=a2, c=c: nc.vector.tensor_tensor(out=arena[:, c, :], in0=pcb[:], in1=a2[:], op=ALU.mult),
                 reads=[pcbk, a2k], writes=[("ar", c)])

    def swa_branch(l, t):
        ub = (l * U_IN + 28) * 128
        for c in range(8):
            if c % 2 == 0:
                w, wk = wload(w_in, ub + (c // 2) * 128)
            p_, pk_ = new_ps()
            proj_fm(w, wk, c % 2, uT_kc, uT_keys, p_, pk_)
            S.op("act", lambda p_=p_, c=c: nc.scalar.activation(out=arena[:, 16 + c, :], in_=p_[:], func=AF.Copy, scale=0.125),
                 reads=[pk_], writes=[("ar", 16 + c)])
        w, wk = wload(w_in, ub + 4 * 128)
        p_, pk_ = new_ps()
        proj_fm(w, wk, 0, uT_kc, uT_keys, p_, pk_)
        S.op("act", lambda p_=p_: nc.scalar.copy(out=skT[l][:, 128:128 + TT], in_=p_[:]), reads=[pk_], writes=[("skT", l)])
        p_, pk_ = new_ps()
        for ch in range(4):
            for kc in range(8):
                mm(p_[:, ch * 128:(ch + 1) * 128], uT[:, kc, ch * 128:(ch + 1) * 128], w[:, (8 + kc) * 128:(9 + kc) * 128],
                   kc == 0, kc == 7, [wk, ("u", kc)], [pk_])
        S.op("act", lambda p_=p_: nc.scalar.copy(out=svt[l][:, 1:5, :], in_=p_[:]), reads=[pk_], writes=[("svt", l)])
        for n in range(4):
            first = (t == 0 and n == 0)
            kbs = (1,) if first else (0, 1)
            for hg in range(2):
                PTs = {}
                for g in range(2):
                    gp = slice(g * 64, (g + 1) * 64)
                    for kb in kbs:
                        pS, pSk = new_ps()
                        mm(pS[:], skT[l][gp, (n + kb) * 128:(n + kb + 1) * 128], arena[gp, 16 + 4 * hg:16 + 4 * hg + 4, n * 128:(n + 1) * 128],
                           True, True, [("skT", l)] + [("ar", 16 + 4 * hg + i) for i in range(4)], [pSk])
                        e, ek = new_tb()
                        S.op("act", lambda pS=pS, e=e: nc.scalar.activation(out=e[:], in_=pS[:], func=AF.Exp), reads=[pSk], writes=[ek])
                        P, Pk = new_tb()
                        S.op("dve", lambda e=e, P=P, kb=kb: nc.vector.tensor_tensor(out=P[:], in0=e[:], in1=cB[:, 192 + kb * 512:192 + (kb + 1) * 512], op=ALU.mult),
                             reads=[ek, "cB"], writes=[Pk])
                        PTs[(g, kb)] = (P, Pk)
                pO, pOk = new_ps()
                pD, pDk = new_ps()
                for g in range(2):
                    gp = slice(g * 64, (g + 1) * 64)
                    for kb in kbs:
                        P, Pk = PTs[(g, kb)]
                        mm(pO[gp, :], svt[l][:, n + kb, gp], P[:], kb == kbs[0], kb == kbs[-1], [("svt", l), Pk], [pOk])
                    for kb in kbs:
                        P, Pk = PTs[(g, kb)]
                        mm(pD[gp, :], ones_b, P[:], kb == kbs[0], kb == kbs[-1], ["cB", Pk], [pDk])
                den, dk = new_tf()
                S.op("dve", lambda pD=pD, den=den, hg=hg: nc.vector.tensor_tensor(out=den[:], in0=pD[:], in1=esb[:, l, 4 * hg:4 * hg + 4, :], op=ALU.add),
                     reads=[pDk, ("esb", l)], writes=[dk])
                rd, rdk = new_tf()
                S.op("dve", lambda den=den, rd=rd: nc.vector.reciprocal(out=rd[:], in_=den[:]), reads=[dk], writes=[rdk])
                S.op("dve", lambda pO=pO, rd=rd, hg=hg, n=n: nc.vector.tensor_tensor(out=arena[:, 4 * hg:4 * hg + 4, n * 128:(n + 1) * 128], in0=pO[:], in1=rd[:], op=ALU.mult),
                     reads=[pOk, rdk], writes=[("ar", 4 * hg + i) for i in range(4)])
        S.op("act", lambda: nc.scalar.copy(out=skT[l][:, 0:128], in_=skT[l][:, TT:TT + 128]), reads=[("skT", l)], writes=[("skT", l)])
        S.op("act", lambda: nc.scalar.copy(out=svt[l][:, 0, :], in_=svt[l][:, 4, :]), reads=[("svt", l)], writes=[("svt", l)])

    def merge_branch(l, b, first, last):
        gbase = (l * U_IN + 33 + b * 4) * 128
        pbase = (l * U_P + b * 4) * 128
        for oc in range(8):
            if oc % 2 == 0:
                wg, wgk = wload(w_in, gbase + (oc // 2) * 128)
                wp, wpk = wload(w_p, pbase + (oc // 2) * 128)
            pg, pgk = new_ps()
            proj_fm(wg, wgk, oc % 2, uT_kc, uT_keys, pg, pgk)
            pp, ppk = new_ps()
            proj_fm(wp, wpk, oc % 2, lambda kc: arena[:, kc, :], [("ar", kc) for kc in range(8)], pp, ppk)
            gt, gtk = new_tf()
            S.op("act", lambda pg=pg, gt=gt: nc.scalar.activation(out=gt[:], in_=pg[:], func=AF.Sigmoid), reads=[pgk], writes=[gtk])
            mk = ("mg", oc)
            if first:
                dst = arena[:, 8 + oc, :] if last else merged[:, oc, :]
                dk_ = ("ar", 8 + oc) if last else mk
                S.op("dve", lambda pp=pp, gt=gt, dst=dst: nc.vector.tensor_tensor(out=dst, in0=pp[:], in1=gt[:], op=ALU.mult),
                     reads=[ppk, gtk], writes=[dk_])
            else:
                tm, tmk = new_tf()
                S.op("dve", lambda pp=pp, gt=gt, tm=tm: nc.vector.tensor_tensor(out=tm[:], in0=pp[:], in1=gt[:], op=ALU.mult),
                     reads=[ppk, gtk], writes=[tmk])
                if last:
                    S.op("dve", lambda tm=tm, oc=oc: nc.vector.tensor_tensor(out=arena[:, 8 + oc, :], in0=merged[:, oc, :], in1=tm[:], op=ALU.add),
                         reads=[mk, tmk], writes=[("ar", 8 + oc)])
                else:
                    S.op("dve", lambda tm=tm, oc=oc: nc.vector.tensor_tensor(out=merged[:, oc, :], in0=merged[:, oc, :], in1=tm[:], op=ALU.add),
                         reads=[mk, tmk], writes=[mk])

    def mixer(par, l, t):
        h = hT[par]
        branches = [b for b, on in ((0, do_ret), (1, do_conv), (2, do_swa)) if on]
        for i, b in enumerate(branches):
            (retention, conv_branch, swa_branch)[b](l, t)
            merge_branch(l, b, i == 0, i == len(branches) - 1)
        for oc in range(8):
            if oc % 2 == 0:
                w, wk = wload(w_p, (l * U_P + 12 + oc // 2) * 128)
            po, pok = new_ps()
            proj_fm(w, wk, oc % 2, lambda kc: arena[:, 8 + kc, :], [("ar", 8 + kc) for kc in range(8)], po, pok)
            S.op("dve", lambda po=po, oc=oc: nc.vector.scalar_tensor_tensor(out=h[:, oc, :], in0=po[:], scalar=modv[:, l, 40 + oc:41 + oc], in1=h[:, oc, :],
                                                                            op0=ALU.mult, op1=ALU.add),
                 reads=[pok, ("mod", l), ("h", par, oc)], writes=[("h", par, oc)])

    out_tokens = []
    for t in range(ntiles):
        par = t % 2
        ts_ = slice(t * TT, (t + 1) * TT)
        S.dma("sp", lambda par=par, ts_=ts_: nc.sync.dma_start(out=hT[par][:], in_=xT[:, :, ts_]),
              writes=[("h", par, kc) for kc in range(8)], chan=("x", par))
        S.dma("sp", lambda ts_=ts_: nc.sync.dma_start(out=ropeS[:], in_=roped[:, :, ts_]), writes=["rope"], chan="rope")
        for l in range(L):
            if do_ffn1:
                norm_to_u(par, l, 0, 0)
                ffn(par, l, 0, 1)
            if do_mix:
                norm_to_u(par, l, 2, 3)
                mixer(par, l, t)
            if do_ffn2:
                norm_to_u(par, l, 3, 6)
                ffn(par, l, 1, 4)
        h = hT[par]
        fb = L * 128

        def out_fn(kc, rstd, rk, h=h, par=par):
            S.op("dve", lambda: nc.vector.scalar_tensor_tensor(out=h[:, kc, :], in0=h[:, kc, :], scalar=vecs[:, fb + kc:fb + kc + 1], in1=rstd[:],
                                                               op0=ALU.mult, op1=ALU.mult),
                 reads=[("h", par, kc), rk, "vecs"], writes=[("h", par, kc)])
        norm_mod(h, lambda kc, par=par: ("h", par, kc), None, None, None, out_fn)
        tok = S.dma("sp", lambda par=par, ts_=ts_: nc.sync.dma_start(out=oT[:, :, ts_], in_=hT[par][:]),
                    reads=[("h", par, kc) for kc in range(8)], chan=("o", par))
        out_tokens.append(tok)
    S.emit(final_tokens=out_tokens[-2:])
    return nc, S


def prepare_inputs(inputs, seq=SEQ, depth=DEPTH, ncore=NCORE):
    f = lambda k: np.asarray(inputs[k], dtype=np.float32)
    x = f("x")
    c = f("c")
    cF, cB, rope, _ = _const_tables(seq)
    L = depth
    w_ada = np.concatenate([_ada_units(f("ada_w")[l]) for l in range(L)], axis=0)
    w_f13 = np.concatenate([_ffn13_units(f(a)[l], f(b)[l]) for l in range(L) for (a, b) in (("ffn1_w1", "ffn1_w3"), ("ffn2_w1", "ffn2_w3"))], axis=0)
    w_f2 = np.concatenate([_ffn2_units(f(a)[l]) for l in range(L) for a in ("ffn1_w2", "ffn2_w2")], axis=0)
    w_in = np.concatenate([_layer_units_in(f("w_in")[l]) for l in range(L)], axis=0)
    w_p = np.concatenate([np.concatenate([_sq_units(f("p_ret")[l]), _sq_units(f("p_conv")[l]), _sq_units(f("p_swa")[l], _SWA_ROWPERM),
                                          _sq_units(f("w_out")[l])], axis=0) for l in range(L)], axis=0)
    vecs = np.zeros((128, L * 128 + 8), np.float32)
    for l in range(L):
        vb = l * 128
        vecs[:, vb:vb + 72] = _fm(f("ada_b")[l])
        vecs[:, vb + 72:vb + 80] = _fm(f("ffn1_norm")[l])
        vecs[:, vb + 80:vb + 88] = _fm(f("mix_norm")[l])
        vecs[:, vb + 88:vb + 96] = _fm(f("ffn2_norm")[l])
        cw = f("conv_w")[l]
        for j in range(3):
            vecs[:, vb + 96 + j * 8:vb + 104 + j * 8] = _fm(cw[j])
        sk = f("swa_sinks")[l]
        for cc in range(8):
            vecs[0:64, vb + 120 + cc] = sk[cc]
            vecs[64:128, vb + 120 + cc] = sk[cc + 8]
    vecs[:, L * 128:L * 128 + 8] = _fm(f("final_norm"))
    shared = dict(vecs=vecs, cF=cF, cB=cB, rope=rope, w_ada=w_ada, w_f13=w_f13, w_f2=w_f2, w_in=w_in, w_p=w_p)
    in_maps = []
    for b in range(ncore):
        m = dict(shared)
        m["xT"] = np.ascontiguousarray(x[b, :seq].T.reshape(8, 128, seq).transpose(1, 0, 2))
        m["cT"] = _fm(c[b])
        in_maps.append(m)
    return in_maps


def kernel(**inputs):
    in_maps = prepare_inputs(inputs)
    nc, _ = build_program()
    res = run_bass_kernel_spmd(nc, in_maps, core_ids=list(range(NCORE)))
    out = np.empty((NCORE, SEQ, D), np.float32)
    for b in range(NCORE):
        o = np.asarray(res.results[b]["oT"])
        out[b] = o.transpose(2, 1, 0).reshape(SEQ, D)
    return out
```

```python
import contextlib
import numpy as np
import concourse.bass as bass
import concourse.mybir as mybir
from concourse.bass_utils import run_bass_kernel_spmd

F32 = mybir.dt.float32
BF16 = mybir.dt.bfloat16
AF = mybir.ActivationFunctionType
ALU = mybir.AluOpType
AX = mybir.AxisListType

D = 1024
SEQ = 4096
NCORE = 8
DEPTH = 2
TT = 512
DFF = 2816
NBLK = DFF // 128
RET_H = 4
EPS = 1e-6
IN_COLS = 11520
OFF_CONV = 4096
OFF_SWA = OFF_CONV + 3072
OFF_GATE = OFF_SWA + 1280
NW = 5
NPS = 6
NTF = 10
NTB = 8

U_ADA = 36
U_F13 = 22
U_F2 = 16
U_IN = 45
U_P = 16


class Sched:
    ENGS = ("pe", "act", "dve", "pool", "sp")

    def __init__(self, nc):
        self.nc = nc
        self.ops = {e: [] for e in self.ENGS}
        self.last_w = {}
        self.readers = {}
        self.chan_n = {}
        self.chan_last = {}

    def _collect(self, reads, writes):
        deps = {}
        for r in reads:
            t = self.last_w.get(r)
            if t is not None:
                deps[t] = True
        for w in writes:
            t = self.last_w.get(w)
            if t is not None and t not in deps:
                deps[t] = False
            for t in self.readers.get(w, {}).values():
                if t not in deps:
                    deps[t] = False
        return deps

    def _commit(self, token, key, reads, writes):
        for r in reads:
            self.readers.setdefault(r, {})[key] = token
        for w in writes:
            self.last_w[w] = token
            self.readers[w] = {}

    def op(self, eng, fn, reads=(), writes=()):
        deps = self._collect(reads, writes)
        idx = len(self.ops[eng])
        token = ("c", eng, idx)
        self.ops[eng].append(dict(fn=fn, deps=deps, kind="c"))
        self._commit(token, eng, reads, writes)
        return token

    def dma(self, q, fn, reads=(), writes=(), chan=None):
        deps = self._collect(reads, writes)
        prev = self.chan_last.get(chan)
        if prev is not None:
            deps.setdefault(prev, True)
        k = self.chan_n.get(chan, 0) + 1
        self.chan_n[chan] = k
        token = ("d", chan, k)
        self.chan_last[chan] = token
        self.ops[q].append(dict(fn=fn, deps=deps, kind="d", chan=chan))
        self._commit(token, ("d", chan), reads, writes)
        return token

    def emit(self, final_tokens=()):
        nc = self.nc
        need = {e: set() for e in self.ENGS}
        for e in self.ENGS:
            for o in self.ops[e]:
                for t, raw in o["deps"].items():
                    if t[0] == "c":
                        if t[1] == e and (e == "pe" or not raw):
                            continue
                        need[t[1]].add(t[2])
        for t in final_tokens:
            if t[0] == "c":
                need[t[1]].add(t[2])
        vals = {}
        for e in self.ENGS:
            c = 0
            for i, o in enumerate(self.ops[e]):
                if o["kind"] == "c" and i in need[e]:
                    c += 1
                    vals[(e, i)] = c
        es = contextlib.ExitStack()
        sem_e = {e: es.enter_context(nc.semaphore("s_" + e)) for e in ("pe", "act", "dve", "pool")}
        sem_c = {ch: es.enter_context(nc.semaphore("d_%d" % i)) for i, ch in enumerate(self.chan_n)}
        handles = {"pe": nc.tensor, "act": nc.scalar, "dve": nc.vector, "pool": nc.gpsimd, "sp": nc.sync}
        self.nwaits = 0

        def run_stream(e, extra_final=False):
            h = handles[e]
            known = {}
            for i, o in enumerate(self.ops[e]):
                for t, raw in o["deps"].items():
                    if t[0] == "c":
                        if t[1] == e and (e == "pe" or not raw):
                            continue
                        sem, v = sem_e[t[1]], vals[(t[1], t[2])]
                        kk = ("c", t[1])
                    else:
                        sem, v = sem_c[t[1]], 16 * t[2]
                        kk = ("d", t[1])
                    if known.get(kk, 0) >= v:
                        continue
                    known[kk] = v
                    h.wait_ge(sem, v)
                    self.nwaits += 1
                ins = o["fn"]()
                if o["kind"] == "d":
                    ins.then_inc(sem_c[o["chan"]], 16)
                elif (e, i) in vals:
                    ins.then_inc(sem_e[e], 1)
            if extra_final:
                for t in final_tokens:
                    if t[0] == "c":
                        h.wait_ge(sem_e[t[1]], vals[(t[1], t[2])])
                    else:
                        h.wait_ge(sem_c[t[1]], 16 * t[2])

        with nc.Block() as block:
            @block.sync
            def _(x):
                run_stream("sp", extra_final=True)

            @block.tensor
            def _(x):
                run_stream("pe")

            @block.scalar
            def _(x):
                run_stream("act")

            @block.vector
            def _(x):
                run_stream("dve")

            @block.gpsimd
            def _(x):
                run_stream("pool")
        es.close()


def _pack_stationary(W, col_chunks, rowperm=None):
    if rowperm is not None:
        W = W[rowperm]
    out = np.empty((len(col_chunks), 128, 8, 128), np.float32)
    Wr = W.reshape(8, 128, W.shape[1])
    for c, cols in enumerate(col_chunks):
        if isinstance(cols, slice):
            blk = Wr[:, :, cols]
        else:
            blk = Wr[:, :, cols]
        out[c] = blk.transpose(1, 0, 2)
    return out


def _pair_units(chunks):
    n2 = chunks.shape[0]
    assert n2 % 2 == 0
    u = chunks.reshape(n2 // 2, 2, 128, 8, 128).transpose(0, 2, 1, 3, 4)
    return u.reshape(n2 // 2 * 128, 2048)


def _sl(a, n=128):
    return slice(a, a + n)


def _swa_q_cols(c):
    return np.concatenate([OFF_SWA + c * 64 + np.arange(64), OFF_SWA + (c + 8) * 64 + np.arange(64)])


_SWA_ROWPERM = np.array([(kc + 8 * (p // 64)) * 64 + p % 64 for kc in range(8) for p in range(128)])


def _layer_units_in(w_in):
    units = []
    for hd in range(RET_H):
        base = hd * 256
        for off in (0, 1024, 3072):
            ch = _pack_stationary(w_in, [_sl(off + base), _sl(off + base + 128)])
            units.append(_pair_units(ch))
        v = w_in[:, 2048 + base:2048 + base + 256].reshape(8, 128, 256).transpose(1, 0, 2)
        units.append(np.ascontiguousarray(v).reshape(128, 2048))
    for c in range(8):
        ch = _pack_stationary(w_in, [_sl(OFF_CONV + 1024 + c * 128), _sl(OFF_CONV + 2048 + c * 128)])
        units.append(_pair_units(ch))
    for c2 in range(4):
        ch = _pack_stationary(w_in, [_sl(OFF_CONV + (2 * c2) * 128), _sl(OFF_CONV + (2 * c2 + 1) * 128)])
        units.append(_pair_units(ch))
    for c2 in range(4):
        ch = _pack_stationary(w_in, [_swa_q_cols(2 * c2), _swa_q_cols(2 * c2 + 1)])
        units.append(_pair_units(ch))
    ch = _pack_stationary(w_in, [_sl(OFF_SWA + 1024), _sl(OFF_SWA + 1024 + 128)])
    units.append(_pair_units(ch))
    for b in range(3):
        for c2 in range(4):
            ch = _pack_stationary(w_in, [_sl(OFF_GATE + b * 1024 + (2 * c2) * 128), _sl(OFF_GATE + b * 1024 + (2 * c2 + 1) * 128)])
            units.append(_pair_units(ch))
    u = np.concatenate(units, axis=0)
    assert u.shape == (U_IN * 128, 2048)
    return u


def _sq_units(W, rowperm=None):
    ch = _pack_stationary(W, [_sl(c * 128) for c in range(8)], rowperm)
    return _pair_units(ch)


def _ffn13_units(w1, w3):
    a = w1.reshape(8, 128, NBLK, 128).transpose(2, 1, 0, 3)
    b = w3.reshape(8, 128, NBLK, 128).transpose(2, 1, 0, 3)
    u = np.stack([a, b], axis=2)
    return np.ascontiguousarray(u).reshape(NBLK * 128, 2048)


def _ffn2_units(w2):
    a = w2.reshape(NBLK, 128, 8, 128).transpose(2, 1, 0, 3)
    a = a.reshape(8, 128, 2, 11, 128).transpose(0, 2, 1, 3, 4)
    return np.ascontiguousarray(a).reshape(16 * 128, 1408)


def _ada_units(ada_w):
    ch = ada_w.reshape(8, 128, 72, 128).transpose(2, 1, 0, 3)
    return _pair_units(np.ascontiguousarray(ch))


def _fm(v):
    return np.ascontiguousarray(v.reshape(-1, 128).T)


def _const_tables(seq):
    h = np.arange(RET_H, dtype=np.float64)
    gam = 1.0 - np.exp2(-5.0 - h)
    i = np.arange(128, dtype=np.float64)
    rel = i[None, :] - i[:, None]
    intraT = np.where(rel[None] >= 0, gam[:, None, None] ** np.maximum(rel[None], 0.0), 0.0)
    xi = gam[None, :] ** (i[:, None] + 1.0)
    zeta = gam[None, :] ** (127.0 - i[:, None])
    cd = [float(np.float32(g ** 128)) for g in gam]
    cF = np.zeros((128, 128 + 512 + 4 + 4 + 1), np.float32)
    cF[:, 0:128] = 1.0
    cF[:, 128:640] = intraT.transpose(1, 0, 2).reshape(128, 512)
    cF[:, 640:644] = xi
    cF[:, 644:648] = zeta
    cF[:, 648] = EPS
    cB = np.zeros((128, 256 + 1024), np.float32)
    cB[:, 0:128] = np.eye(128)
    cB[:, 128:256] = 1.0
    j = np.arange(128)[:, None]
    ii = np.arange(128)[None, :]
    mprev = (j > ii).astype(np.float32)
    mcur = (j <= ii).astype(np.float32)
    cB[:, 256:256 + 512] = np.tile(mprev, (1, 4))
    cB[:, 256 + 512:256 + 1024] = np.tile(mcur, (1, 4))
    inv = np.power(np.float32(10000.0), -np.linspace(0.0, 1.0, 128, dtype=np.float32)).astype(np.float32)
    ang = (np.arange(seq, dtype=np.float32)[:, None] * inv[None, :]).astype(np.float32)
    rope = np.stack([np.cos(ang.astype(np.float64)).T, np.sin(ang.astype(np.float64)).T], axis=1).astype(np.float32)
    return cF, cB, np.ascontiguousarray(rope), cd


def build_program(seq=SEQ, depth=DEPTH, cfg=None):
    cfg = cfg or {}
    do_ffn1 = cfg.get("ffn1", True)
    do_mix = cfg.get("mix", True)
    do_ffn2 = cfg.get("ffn2", True)
    do_ret = cfg.get("ret", True)
    do_conv = cfg.get("conv", True)
    do_swa = cfg.get("swa", True)
    ntiles = seq // TT
    L = depth
    cd = _const_tables(128)[3]

    nc = bass.Bass("TRN2", target_bir_lowering=False)
    xT = nc.dram_tensor("xT", [128, 8, seq], F32, kind="ExternalInput").ap()
    cTd = nc.dram_tensor("cT", [128, 8], F32, kind="ExternalInput").ap()
    vecsd = nc.dram_tensor("vecs", [128, L * 128 + 8], F32, kind="ExternalInput").ap()
    cFd = nc.dram_tensor("cF", [128, 649], F32, kind="ExternalInput").ap()
    cBd = nc.dram_tensor("cB", [128, 1280], F32, kind="ExternalInput").ap()
    roped = nc.dram_tensor("rope", [128, 2, seq], F32, kind="ExternalInput").ap()
    w_ada = nc.dram_tensor("w_ada", [L * U_ADA * 128, 2048], F32, kind="ExternalInput").ap()
    w_f13 = nc.dram_tensor("w_f13", [L * 2 * U_F13 * 128, 2048], F32, kind="ExternalInput").ap()
    w_f2 = nc.dram_tensor("w_f2", [L * 2 * U_F2 * 128, 1408], F32, kind="ExternalInput").ap()
    w_in = nc.dram_tensor("w_in", [L * U_IN * 128, 2048], F32, kind="ExternalInput").ap()
    w_p = nc.dram_tensor("w_p", [L * U_P * 128, 2048], F32, kind="ExternalInput").ap()
    oT = nc.dram_tensor("oT", [128, 8, seq], F32, kind="ExternalOutput").ap()

    S = Sched(nc)
    A = nc.alloc_sbuf_tensor
    hT = [A("hT%d" % i, [128, 8, TT], F32) for i in range(2)]
    uT = A("uT", [128, 8, TT], BF16)
    wb = [A("wb%d" % i, [128, 2048], BF16) for i in range(NW)]
    arena = A("arena", [128, 24, TT], BF16)
    tF = [A("tF%d" % i, [128, TT], F32) for i in range(NTF)]
    tB = [A("tB%d" % i, [128, TT], BF16) for i in range(NTB)]
    ropeS = A("ropeS", [128, 2, TT], F32)
    S32 = A("S32", [128, L * RET_H, 512], F32)
    Sbv = [A("Sbv%d" % i, [128, 4, 512], BF16) for i in range(2)]
    merged = A("merged", [128, 8, TT], F32)
    qTr = [A("qTr%d" % i, [128, 2, TT], BF16) for i in range(2)]
    kTr = [A("kTr%d" % i, [128, 2, TT], BF16) for i in range(2)]
    gS = [A("gS%d" % i, [128, 2, TT], BF16) for i in range(2)]
    vtok = [A("vtok%d" % i, [128, 4, 256], BF16) for i in range(2)]
    ktok = [A("ktok%d" % i, [128, 4, 256], BF16) for i in range(2)]
    PT4 = [A("PT4%d" % i, [128, 4, 128], BF16) for i in range(2)]
    cxs = [A("cxs%d" % i, [128, 256], F32) for i in range(2)]
    otok = [A("otok%d" % i, [128, 256], F32) for i in range(2)]
    ontok = [A("ontok%d" % i, [128, 4, 256], BF16) for i in range(2)]
    junk = A("junk", [128, 256], F32)
    st = [A("st%d" % i, [128, 4], F32) for i in range(2)]
    zb = [A("zb%d" % i, [128, TT + 2], F32) for i in range(2)]
    convst = A("convst", [128, L * 8, 2], F32)
    skT = [A("skT%d" % l, [128, 128 + TT], BF16) for l in range(L)]
    svt = [A("svt%d" % l, [128, 5, 128], BF16) for l in range(L)]
    cF = A("cFs", [128, 649], F32)
    cB = A("cBs", [128, 1280], BF16)
    vecs = A("vecss", [128, L * 128 + 8], F32)
    cT = A("cTs", [128, 8], F32)
    cact = A("cact", [128, 8], BF16)
    modv = A("modv", [128, L, 72], F32)
    der = A("der", [128, L, 5, 8], F32)
    es = A("es", [128, L, 8], F32)
    ps = [nc.alloc_psum_tensor("ps%d" % i, [128, 512], F32) for i in range(NPS)]
    pt = [nc.alloc_psum_tensor("pt%d" % i, [128, 1024], BF16) for i in range(2)]

    ones_f = cF[:, 0:128]
    eps_ap = cF[:, 648:649]
    ident = cB[:, 0:128]
    ones_b = cB[:, 128:192]
    ones_b128 = cB[:, 128:256]

    ctr = dict(ps=0, tf=0, tb=0, wb=0, pt=0)

    def new_ps():
        i = ctr["ps"] % NPS
        ctr["ps"] += 1
        return ps[i], ("ps", i)

    def new_pt():
        i = ctr["pt"] % 2
        ctr["pt"] += 1
        return pt[i], ("pt", i)

    def new_tf():
        i = ctr["tf"] % NTF
        ctr["tf"] += 1
        return tF[i], ("tf", i)

    def new_tb():
        i = ctr["tb"] % NTB
        ctr["tb"] += 1
        return tB[i], ("tb", i)

    def wload(dram, row0, ncols=2048):
        k = ctr["wb"] % NW
        ctr["wb"] += 1
        src = dram[row0:row0 + 128, 0:ncols]
        S.dma("pool", lambda: nc.gpsimd.dma_start(out=wb[k][:, 0:ncols], in_=src), writes=[("wb", k)], chan=("wb", k))
        return wb[k], ("wb", k)

    def mm(out, lhsT, rhs, start, stop, reads, writes):
        S.op("pe", lambda: nc.tensor.matmul(out, lhsT=lhsT, rhs=rhs, start=start, stop=stop), reads=reads, writes=writes)

    def proj_fm(w, wk, o, rhs_of_kc, rkeys, pst, pk, cols=slice(0, TT)):
        for kc in range(8):
            mm(pst[:, cols], w[:, (o * 8 + kc) * 128:(o * 8 + kc + 1) * 128], rhs_of_kc(kc), kc == 0, kc == 7,
               [wk, rkeys[kc]], [pk])

    uT_kc = lambda kc: uT[:, kc, :]
    uT_keys = [("u", kc) for kc in range(8)]

    S.dma("sp", lambda: nc.sync.dma_start(out=cF[:], in_=cFd), writes=["cF"], chan="c0")
    S.dma("sp", lambda: nc.sync.dma_start(out=vecs[:], in_=vecsd), writes=["vecs"], chan="c1")
    S.dma("sp", lambda: nc.sync.dma_start(out=cT[:], in_=cTd), writes=["cT"], chan="c2")
    S.dma("pool", lambda: nc.gpsimd.dma_start(out=cB[:], in_=cBd), writes=["cB"], chan="c3")
    S.op("act", lambda: nc.scalar.activation(out=cact[:], in_=cT[:], func=AF.Silu), reads=["cT"], writes=["cact"])
    S.op("dve", lambda: nc.vector.memset(S32[:], 0.0), writes=[("S32", i) for i in range(L * RET_H)])
    S.op("dve", lambda: nc.vector.memset(convst[:], 0.0), writes=[("cst", i) for i in range(L * 8)])
    for l in range(L):
        S.op("dve", lambda l=l: nc.vector.memset(skT[l][:], 0.0), writes=[("skT", l)])
        S.op("dve", lambda l=l: nc.vector.memset(svt[l][:], 0.0), writes=[("svt", l)])
    for l in range(L):
        pm, pmk = new_ps()
        for u in range(U_ADA):
            w, wk = wload(w_ada, (l * U_ADA + u) * 128)
            for o in range(2):
                oc = 2 * u + o
                for kc in range(8):
                    mm(pm[:, oc:oc + 1], w[:, (o * 8 + kc) * 128:(o * 8 + kc + 1) * 128], cact[:, kc:kc + 1],
                       kc == 0, kc == 7, [wk, "cact"], [pmk])
        vb = l * 128
        S.op("dve", lambda l=l, pm=pm, vb=vb: nc.vector.tensor_tensor(out=modv[:, l, :], in0=pm[:, 0:72], in1=vecs[:, vb:vb + 72], op=ALU.add),
             reads=[pmk, "vecs"], writes=[("mod", l)])
        for di, (sc_i, nv) in enumerate(((1, 72), (4, 80), (7, 88))):
            dst = (0, 2, 3)[di]
            S.op("dve", lambda l=l, dst=dst, sc_i=sc_i, nv=nv, vb=vb: nc.vector.scalar_tensor_tensor(
                out=der[:, l, dst, :], in0=modv[:, l, sc_i * 8:(sc_i + 1) * 8], scalar=1.0, in1=vecs[:, vb + nv:vb + nv + 8],
                op0=ALU.add, op1=ALU.mult), reads=[("mod", l), "vecs"], writes=[("der", l)])
        for dst, g_i in ((1, 2), (4, 8)):
            S.op("dve", lambda l=l, dst=dst, g_i=g_i: nc.vector.tensor_scalar(
                out=der[:, l, dst, :], in0=modv[:, l, g_i * 8:(g_i + 1) * 8], scalar1=0.5, scalar2=None, op0=ALU.mult),
                reads=[("mod", l)], writes=[("der", l)])
        S.op("act", lambda l=l, vb=vb: nc.scalar.activation(out=es[:, l, :], in_=vecs[:, vb + 120:vb + 128], func=AF.Exp),
             reads=["vecs"], writes=[("es", l)])

    def norm_mod(h, hk, a_of, b_of, extra_reads, out_fn):
        pss, pk = new_ps()
        for kc in range(8):
            sq, sk = new_tb()
            S.op("act", lambda kc=kc, sq=sq: nc.scalar.activation(out=sq[:], in_=h[:, kc, :], func=AF.Square),
                 reads=[hk(kc)], writes=[sk])
            mm(pss[:], ones_b128, sq[:], kc == 0, kc == 7, [sk, "cB"], [pk])
        std, stk = new_tf()
        S.op("act", lambda: nc.scalar.activation(out=std[:], in_=pss[:], func=AF.Ln, scale=1.0 / D, bias=eps_ap),
             reads=[pk, "cF"], writes=[stk])
        rstd, rk = new_tf()
        S.op("act", lambda: nc.scalar.activation(out=rstd[:], in_=std[:], func=AF.Exp, scale=-0.5), reads=[stk], writes=[rk])
        for kc in range(8):
            out_fn(kc, rstd, rk)

    def norm_to_u(par, l, a_idx, sh_idx):
        h = hT[par]
        hk = lambda kc: ("h", par, kc)

        def out_fn(kc, rstd, rk):
            nt, nk = new_tf()
            S.op("dve", lambda: nc.vector.scalar_tensor_tensor(out=nt[:], in0=h[:, kc, :], scalar=der[:, l, a_idx, kc:kc + 1], in1=rstd[:],
                                                               op0=ALU.mult, op1=ALU.mult),
                 reads=[hk(kc), rk, ("der", l)], writes=[nk])
            S.op("act", lambda: nc.scalar.activation(out=uT[:, kc, :], in_=nt[:], func=AF.Identity,
                                                      bias=modv[:, l, sh_idx * 8 + kc:sh_idx * 8 + kc + 1], scale=1.0),
                 reads=[nk, ("mod", l)], writes=[("u", kc)])
        norm_mod(h, hk, None, None, None, out_fn)

    def ffn(par, l, which, gs_idx):
        h = hT[par]
        b13 = ((l * 2 + which) * U_F13) * 128
        b2 = ((l * 2 + which) * U_F2) * 128
        for blk in range(NBLK):
            w, wk = wload(w_f13, b13 + blk * 128)
            pA, pAk = new_ps()
            pB, pBk = new_ps()
            proj_fm(w, wk, 0, uT_kc, uT_keys, pA, pAk)
            proj_fm(w, wk, 1, uT_kc, uT_keys, pB, pBk)
            s, sk = new_tf()
            S.op("act", lambda pA=pA, s=s: nc.scalar.activation(out=s[:], in_=pA[:], func=AF.Silu), reads=[pAk], writes=[sk])
            S.op("dve", lambda pB=pB, s=s, blk=blk: nc.vector.tensor_tensor(out=arena[:, blk, :], in0=pB[:], in1=s[:], op=ALU.mult),
                 reads=[pBk, sk], writes=[("ar", blk)])
        for oc in range(8):
            pC, pCk = new_ps()
            for half in range(2):
                w, wk = wload(w_f2, b2 + (oc * 2 + half) * 128, 1408)
                for j in range(11):
                    blk = half * 11 + j
                    mm(pC[:], w[:, j * 128:(j + 1) * 128], arena[:, blk, :], blk == 0, blk == NBLK - 1, [wk, ("ar", blk)], [pCk])
            S.op("dve", lambda pC=pC, oc=oc: nc.vector.scalar_tensor_tensor(out=h[:, oc, :], in0=pC[:], scalar=der[:, l, gs_idx, oc:oc + 1],
                                                                            in1=h[:, oc, :], op0=ALU.mult, op1=ALU.add),
                 reads=[pCk, ("der", l), ("h", par, oc)], writes=[("h", par, oc)])

    def rope_evac(pq, pqk, dst, dstk, kscale):
        cos = ropeS[:, 0, :]
        sin = ropeS[:, 1, :]
        ts = []
        for (src, tab) in ((0, cos), (1, sin), (1, cos), (0, sin)):
            t, tk = new_tf()
            if kscale is None:
                S.op("dve", lambda t=t, src=src, tab=tab: nc.vector.tensor_tensor(out=t[:], in0=pq[src][:], in1=tab, op=ALU.mult),
                     reads=[pqk[src], "rope"], writes=[tk])
            else:
                S.op("dve", lambda t=t, src=src, tab=tab: nc.vector.scalar_tensor_tensor(out=t[:], in0=pq[src][:], scalar=kscale, in1=tab,
                                                                                        op0=ALU.mult, op1=ALU.mult),
                     reads=[pqk[src], "rope"], writes=[tk])
            ts.append((t, tk))
        S.op("dve", lambda: nc.vector.tensor_tensor(out=dst[:, 0, :], in0=ts[0][0][:], in1=ts[1][0][:], op=ALU.subtract),
             reads=[ts[0][1], ts[1][1]], writes=[dstk])
        S.op("dve", lambda: nc.vector.tensor_tensor(out=dst[:, 1, :], in0=ts[2][0][:], in1=ts[3][0][:], op=ALU.add),
             reads=[ts[2][1], ts[3][1]], writes=[dstk])

    def ret_proj(l, hd):
        ub = (l * U_IN) * 128
        sset = hd % 2
        q_d, k_d, g_d = qTr[sset], kTr[sset], gS[sset]
        qk_, kk_, gk_ = ("qTr", sset), ("kTr", sset), ("gS", sset)
        vt, vtk = vtok[sset], ("vtok", sset)
        kt, ktk = ktok[sset], ("ktok", sset)
        for part, (dst, dstk, ksc) in enumerate(((q_d, qk_, None), (k_d, kk_, 1.0 / 16.0))):
            w, wk = wload(w_in, ub + (hd * 4 + part) * 128)
            pq = []
            pqk = []
            for dc in range(2):
                p_, pk_ = new_ps()
                proj_fm(w, wk, dc, uT_kc, uT_keys, p_, pk_)
                pq.append(p_)
                pqk.append(pk_)
            rope_evac(pq, pqk, dst, dstk, ksc)
            yield
        w, wk = wload(w_in, ub + (hd * 4 + 2) * 128)
        for dc in range(2):
            p_, pk_ = new_ps()
            proj_fm(w, wk, dc, uT_kc, uT_keys, p_, pk_)
            S.op("act", lambda p_=p_, dc=dc, g_d=g_d: nc.scalar.activation(out=g_d[:, dc, :], in_=p_[:], func=AF.Silu),
                 reads=[pk_], writes=[gk_])
        yield
        w, wk = wload(w_in, ub + (hd * 4 + 3) * 128)
        for c2 in range(2):
            p_, pk_ = new_ps()
            for cc in range(2):
                ch = 2 * c2 + cc
                for kc in range(8):
                    mm(p_[:, cc * 256:(cc + 1) * 256], uT[:, kc, ch * 128:(ch + 1) * 128], w[:, kc * 256:(kc + 1) * 256],
                       kc == 0, kc == 7, [wk, ("u", kc)], [pk_])
            S.op("act", lambda p_=p_, c2=c2, vt=vt: nc.scalar.copy(out=vt[:, 2 * c2:2 * c2 + 2, :], in_=p_[:]),
                 reads=[pk_], writes=[vtk])
        pT, pTk = new_pt()
        for ch in range(4):
            for dc in range(2):
                S.op("pe", lambda ch=ch, dc=dc, pT=pT, k_d=k_d: nc.tensor.transpose(
                    out=pT[:, ch * 256 + dc * 128:ch * 256 + (dc + 1) * 128], in_=k_d[:, dc, ch * 128:(ch + 1) * 128], identity=ident),
                    reads=[kk_, "cB"], writes=[pTk])
        S.op("act", lambda pT=pT, kt=kt, hd=hd: nc.scalar.activation(out=kt[:], in_=pT[:], func=AF.Copy, scale=cF[:, 644 + hd:645 + hd]),
             reads=[pTk, "cF"], writes=[ktk])
        yield

    def ret_recur(l, hd, other, finish_prev=None):
        sset = hd % 2
        sidx = l * RET_H + hd
        q_d, k_d, g_d = qTr[sset], kTr[sset], gS[sset]
        qk_, kk_, gk_ = ("qTr", sset), ("kTr", sset), ("gS", sset)
        vt, vtk = vtok[sset], ("vtok", sset)
        kt, ktk = ktok[sset], ("ktok", sset)
        sv = Sbv[sset]
        svk = lambda c: ("Sbv", sset, c)
        P4, P4k = PT4[sset], ("PT4", sset)

        def step():
            if other is not None:
                next(other, None)

        S.op("act", lambda: nc.scalar.copy(out=sv[:, 0, :], in_=S32[:, sidx, :]), reads=[("S32", sidx)], writes=[svk(0)])
        pS, pSk = new_ps()
        for ch in range(4):
            cs = slice(ch * 128, (ch + 1) * 128)
            for dc in range(2):
                mm(pS[:, cs], k_d[:, dc, cs], q_d[:, dc, cs], dc == 0, dc == 1, [kk_, qk_], [pSk])
        S.op("dve", lambda: nc.vector.tensor_tensor(out=P4[:], in0=pS[:].rearrange("p (a b) -> p a b", a=4),
                                                    in1=cF[:, 128 + hd * 128:256 + hd * 128].unsqueeze(1).broadcast_to([128, 4, 128]), op=ALU.mult),
             reads=[pSk, "cF"], writes=[P4k])
        for ch in range(4):
            pU, pUk = new_ps()
            for dc in range(2):
                mm(pU[:, dc * 256:(dc + 1) * 256], kt[:, ch, dc * 128:(dc + 1) * 128], vt[:, ch, :], True, True, [ktk, vtk], [pUk])
            S.op("dve", lambda pU=pU: nc.vector.scalar_tensor_tensor(out=S32[:, sidx, :], in0=S32[:, sidx, :], scalar=cd[hd], in1=pU[:],
                                                                     op0=ALU.mult, op1=ALU.add),
                 reads=[pUk, ("S32", sidx)], writes=[("S32", sidx)])
            if ch < 3:
                S.op("act", lambda ch=ch: nc.scalar.copy(out=sv[:, ch + 1, :], in_=S32[:, sidx, :]), reads=[("S32", sidx)], writes=[svk(ch + 1)])
        if finish_prev is not None:
            finish_prev()
        for ch in range(4):
            step()
            cs = slice(ch * 128, (ch + 1) * 128)
            b = ch % 2
            pO, pOk = new_ps()
            mm(pO[:, 0:256], P4[:, ch, :], vt[:, ch, :], True, True, [P4k, vtk], [pOk])
            for dc in range(2):
                mm(pO[:, 256:512], q_d[:, dc, cs], sv[:, ch, dc * 256:(dc + 1) * 256], dc == 0, dc == 1, [qk_, svk(ch)], [pOk])
            S.op("act", lambda pO=pO, b=b: nc.scalar.activation(out=cxs[b][:], in_=pO[:, 256:512], func=AF.Copy, scale=cF[:, 640 + hd:641 + hd]),
                 reads=[pOk, "cF"], writes=[("cxs", b)])
            S.op("dve", lambda pO=pO, b=b: nc.vector.tensor_tensor(out=otok[b][:], in0=pO[:, 0:256], in1=cxs[b][:], op=ALU.add),
                 reads=[pOk, ("cxs", b)], writes=[("otok", b)])
            S.op("dve", lambda b=b: nc.vector.tensor_tensor(out=junk[:], in0=otok[b][:], in1=otok[b][:], op=ALU.mult),
                 reads=[("otok", b)], writes=["junk"])
            S.op("dve", lambda b=b: nc.vector.tensor_reduce(out=st[b][:, 0:1], in_=junk[:], axis=AX.X, op=ALU.add),
                 reads=["junk"], writes=[("st0", b)])
            S.op("act", lambda b=b: nc.scalar.activation(out=st[b][:, 1:2], in_=st[b][:, 0:1], func=AF.Ln, scale=1.0 / 256, bias=eps_ap),
                 reads=[("st0", b), "cF"], writes=[("st1", b)])
            S.op("act", lambda b=b: nc.scalar.activation(out=st[b][:, 2:3], in_=st[b][:, 1:2], func=AF.Exp, scale=-0.5), reads=[("st1", b)], writes=[("st2", b)])
            S.op("act", lambda b=b, ch=ch: nc.scalar.activation(out=ontok[sset][:, ch, :], in_=otok[b][:], func=AF.Copy, scale=st[b][:, 2:3]),
                 reads=[("otok", b), ("st2", b)], writes=[("ontok", sset, ch)])

    def ret_finish(l, hd):
        sset = hd % 2
        g_d, gk_ = gS[sset], ("gS", sset)
        pY, pYk = new_pt()
        for ch in range(4):
            for ec in range(2):
                S.op("pe", lambda ec=ec, ch=ch: nc.tensor.transpose(
                    out=pY[:, ec * 512 + ch * 128:ec * 512 + (ch + 1) * 128], in_=ontok[sset][:, ch, ec * 128:(ec + 1) * 128], identity=ident),
                    reads=[("ontok", sset, ch), "cB"], writes=[pYk])
        for ec in range(2):
            S.op("dve", lambda ec=ec: nc.vector.tensor_tensor(out=arena[:, 2 * hd + ec, :], in0=pY[:, ec * 512:(ec + 1) * 512],
                                                              in1=g_d[:, ec, :], op=ALU.mult),
                 reads=[pYk, gk_], writes=[("ar", 2 * hd + ec)])

    def retention(l, t):
        gens = [ret_proj(l, hd) for hd in range(RET_H)]
        for _ in gens[0]:
            pass
        for hd in range(RET_H):
            other = gens[hd + 1] if hd + 1 < RET_H else None
            fp = (lambda hd=hd: ret_finish(l, hd - 1)) if hd > 0 else None
            ret_recur(l, hd, other, fp)
            if other is not None:
                for _ in other:
                    pass
        ret_finish(l, RET_H - 1)
        yield

    def conv_branch(l, t, ybase):
        ub = (l * U_IN + 16) * 128
        vb = l * 128 + 96
        wcb = None
        for c in range(8):
            w, wk = wload(w_in, ub + c * 128)
            pcc, pcck = new_ps()
            pcx, pcxk = new_ps()
            proj_fm(w, wk, 0, uT_kc, uT_keys, pcc, pcck)
            proj_fm(w, wk, 1, uT_kc, uT_keys, pcx, pcxk)
            ccs, ccsk = new_tf()
            S.op("act", lambda pcc=pcc, ccs=ccs: nc.scalar.copy(out=ccs[:], in_=pcc[:]), reads=[pcck], writes=[ccsk])
            z = zb[c % 2]
            zk = ("zb", c % 2)
            ck = ("cst", l * 8 + c)
            S.op("act", lambda z=z, c=c: nc.scalar.copy(out=z[:, 0:2], in_=convst[:, l * 8 + c, :]), reads=[ck], writes=[zk])
            S.op("dve", lambda z=z, pcx=pcx, ccs=ccs: nc.vector.tensor_tensor(out=z[:, 2:TT + 2], in0=pcx[:], in1=ccs[:], op=ALU.mult),
                 reads=[pcxk, ccsk], writes=[zk])
            S.op("act", lambda z=z, c=c: nc.scalar.copy(out=convst[:, l * 8 + c, :], in_=z[:, TT:TT + 2]), reads=[zk], writes=[ck])
            a0, a0k = new_tf()
            S.op("act", lambda z=z, a0=a0, c=c: nc.scalar.activation(out=a0[:], in_=z[:, 0:TT], func=AF.Copy, scale=vecs[:, vb + c:vb + c + 1]),
                 reads=[zk, "vecs"], writes=[a0k])
            a1, a1k = new_tf()
            S.op("dve", lambda z=z, a0=a0, a1=a1, c=c: nc.vector.scalar_tensor_tensor(out=a1[:], in0=z[:, 1:TT + 1], scalar=vecs[:, vb + 8 + c:vb + 9 + c], in1=a0[:],
                                                                                   op0=ALU.mult, op1=ALU.add),
                 reads=[zk, "vecs", a0k], writes=[a1k])
            a2, a2k = new_tf()
            S.op("dve", lambda z=z, a1=a1, a2=a2, c=c: nc.vector.scalar_tensor_tensor(out=a2[:], in0=z[:, 2:TT + 2], scalar=vecs[:, vb + 16 + c:vb + 17 + c], in1=a1[:],
                                                                                   op0=ALU.mult, op1=ALU.add),
                 reads=[zk, "vecs", a1k], writes=[a2k])
            if c % 2 == 0:
                wcb, wcbk = wload(w_in, (l * U_IN + 24 + c // 2) * 128)
            pcb, pcbk = new_ps()
            proj_fm(wcb, wcbk, c % 2, uT_kc, uT_keys, pcb, pcbk)
            S.op("dve", lambda pcb=pcb, a2=a2, c=c: nc.vector.tensor_tensor(out=arena[:, ybase + c, :], in0=pcb[:], in1=a2[:], op=ALU.mult),
                 reads=[pcbk, a2k], writes=[("ar", ybase + c)])
            yield

    def swa_branch(l, t, ybase):
        ub = (l * U_IN + 28) * 128
        for c in range(8):
            if c % 2 == 0:
                w, wk = wload(w_in, ub + (c // 2) * 128)
            p_, pk_ = new_ps()
            proj_fm(w, wk, c % 2, uT_kc, uT_keys, p_, pk_)
            S.op("act", lambda p_=p_, c=c: nc.scalar.activation(out=arena[:, 16 + c, :], in_=p_[:], func=AF.Copy, scale=0.125),
                 reads=[pk_], writes=[("ar", 16 + c)])
            if c % 2 == 1:
                yield
        w, wk = wload(w_in, ub + 4 * 128)
        p_, pk_ = new_ps()
        proj_fm(w, wk, 0, uT_kc, uT_keys, p_, pk_)
        S.op("act", lambda p_=p_: nc.scalar.copy(out=skT[l][:, 128:128 + TT], in_=p_[:]), reads=[pk_], writes=[("skT", l)])
        p_, pk_ = new_ps()
        for ch in range(4):
            for kc in range(8):
                mm(p_[:, ch * 128:(ch + 1) * 128], uT[:, kc, ch * 128:(ch + 1) * 128], w[:, (8 + kc) * 128:(9 + kc) * 128],
                   kc == 0, kc == 7, [wk, ("u", kc)], [pk_])
        S.op("act", lambda p_=p_: nc.scalar.copy(out=svt[l][:, 1:5, :], in_=p_[:]), reads=[pk_], writes=[("svt", l)])
        yield
        for n in range(4):
            first = (t == 0 and n == 0)
            kbs = (1,) if first else (0, 1)
            for hg in range(2):
                PTs = {}
                for g in range(2):
                    gp = slice(g * 64, (g + 1) * 64)
                    for kb in kbs:
                        pS, pSk = new_ps()
                        mm(pS[:], skT[l][gp, (n + kb) * 128:(n + kb + 1) * 128], arena[gp, 16 + 4 * hg:16 + 4 * hg + 4, n * 128:(n + 1) * 128],
                           True, True, [("skT", l)] + [("ar", 16 + 4 * hg + i) for i in range(4)], [pSk])
                        e, ek = new_tb()
                        S.op("act", lambda pS=pS, e=e: nc.scalar.activation(out=e[:], in_=pS[:], func=AF.Exp), reads=[pSk], writes=[ek])
                        P, Pk = new_tb()
                        S.op("dve", lambda e=e, P=P, kb=kb: nc.vector.tensor_tensor(out=P[:], in0=e[:], in1=cB[:, 256 + kb * 512:256 + (kb + 1) * 512], op=ALU.mult),
                             reads=[ek, "cB"], writes=[Pk])
                        PTs[(g, kb)] = (P, Pk)
                pO, pOk = new_ps()
                pD, pDk = new_ps()
                for g in range(2):
                    gp = slice(g * 64, (g + 1) * 64)
                    for kb in kbs:
                        P, Pk = PTs[(g, kb)]
                        mm(pO[gp, :], svt[l][:, n + kb, gp], P[:], kb == kbs[0], kb == kbs[-1], [("svt", l), Pk], [pOk])
                    for kb in kbs:
                        P, Pk = PTs[(g, kb)]
                        mm(pD[gp, :], ones_b, P[:], kb == kbs[0], kb == kbs[-1], ["cB", Pk], [pDk])
                den, dk = new_tf()
                S.op("dve", lambda pD=pD, den=den, hg=hg: nc.vector.tensor_tensor(out=den[:].rearrange("p (a b) -> p a b", a=4), in0=pD[:].rearrange("p (a b) -> p a b", a=4), in1=es[:, l, 4 * hg:4 * hg + 4].unsqueeze(2).broadcast_to([128, 4, 128]), op=ALU.add),
                     reads=[pDk, ("es", l)], writes=[dk])
                lnd, lndk = new_tf()
                S.op("act", lambda den=den, lnd=lnd: nc.scalar.activation(out=lnd[:], in_=den[:], func=AF.Ln), reads=[dk], writes=[lndk])
                rd, rdk = new_tf()
                S.op("act", lambda lnd=lnd, rd=rd: nc.scalar.activation(out=rd[:], in_=lnd[:], func=AF.Exp, scale=-1.0), reads=[lndk], writes=[rdk])
                S.op("dve", lambda pO=pO, rd=rd, hg=hg, n=n: nc.vector.tensor_tensor(out=arena[:, ybase + 4 * hg:ybase + 4 * hg + 4, n * 128:(n + 1) * 128], in0=pO[:], in1=rd[:], op=ALU.mult),
                     reads=[pOk, rdk], writes=[("ar", ybase + 4 * hg + i) for i in range(4)])
                yield
        S.op("act", lambda: nc.scalar.copy(out=skT[l][:, 0:128], in_=skT[l][:, TT:TT + 128]), reads=[("skT", l)], writes=[("skT", l)])
        S.op("act", lambda: nc.scalar.copy(out=svt[l][:, 0, :], in_=svt[l][:, 4, :]), reads=[("svt", l)], writes=[("svt", l)])

    def merge_branch(l, b, first, last, ybase):
        gbase = (l * U_IN + 33 + b * 4) * 128
        pbase = (l * U_P + b * 4) * 128
        for oc in range(8):
            if oc % 2 == 0:
                wg, wgk = wload(w_in, gbase + (oc // 2) * 128)
                wp, wpk = wload(w_p, pbase + (oc // 2) * 128)
            pg, pgk = new_ps()
            proj_fm(wg, wgk, oc % 2, uT_kc, uT_keys, pg, pgk)
            pp, ppk = new_ps()
            proj_fm(wp, wpk, oc % 2, lambda kc: arena[:, ybase + kc, :], [("ar", ybase + kc) for kc in range(8)], pp, ppk)
            gt, gtk = new_tf()
            S.op("act", lambda pg=pg, gt=gt: nc.scalar.activation(out=gt[:], in_=pg[:], func=AF.Sigmoid), reads=[pgk], writes=[gtk])
            mk = ("mg", oc)
            if first:
                dst = arena[:, 8 + oc, :] if last else merged[:, oc, :]
                dk_ = ("ar", 8 + oc) if last else mk
                S.op("dve", lambda pp=pp, gt=gt, dst=dst: nc.vector.tensor_tensor(out=dst, in0=pp[:], in1=gt[:], op=ALU.mult),
                     reads=[ppk, gtk], writes=[dk_])
            else:
                tm, tmk = new_tf()
                S.op("dve", lambda pp=pp, gt=gt, tm=tm: nc.vector.tensor_tensor(out=tm[:], in0=pp[:], in1=gt[:], op=ALU.mult),
                     reads=[ppk, gtk], writes=[tmk])
                if last:
                    S.op("dve", lambda tm=tm, oc=oc: nc.vector.tensor_tensor(out=arena[:, 8 + oc, :], in0=merged[:, oc, :], in1=tm[:], op=ALU.add),
                         reads=[mk, tmk], writes=[("ar", 8 + oc)])
                else:
                    S.op("dve", lambda tm=tm, oc=oc: nc.vector.tensor_tensor(out=merged[:, oc, :], in0=merged[:, oc, :], in1=tm[:], op=ALU.add),
                         reads=[mk, tmk], writes=[mk])
            yield

    def interleave(ga, gb):
        ga = iter(ga) if ga is not None else iter(())
        gb = iter(gb) if gb is not None else iter(())
        da = db = False
        while not (da and db):
            if not da:
                try:
                    next(ga)
                except StopIteration:
                    da = True
            if not db:
                try:
                    next(gb)
                except StopIteration:
                    db = True

    def mixer(par, l, t):
        h = hT[par]
        bgen = {0: lambda: retention(l, t), 1: lambda: conv_branch(l, t, 8), 2: lambda: swa_branch(l, t, 0)}
        ybase = {0: 0, 1: 8, 2: 0}
        branches = [b for b, on in ((0, do_ret), (1, do_conv), (2, do_swa)) if on]
        prev_merge = None
        for i, b in enumerate(branches):
            interleave(bgen[b](), prev_merge)
            prev_merge = merge_branch(l, b, i == 0, i == len(branches) - 1, ybase[b])
        interleave(prev_merge, None)
        for oc in range(8):
            if oc % 2 == 0:
                w, wk = wload(w_p, (l * U_P + 12 + oc // 2) * 128)
            po, pok = new_ps()
            proj_fm(w, wk, oc % 2, lambda kc: arena[:, 8 + kc, :], [("ar", 8 + kc) for kc in range(8)], po, pok)
            S.op("dve", lambda po=po, oc=oc: nc.vector.scalar_tensor_tensor(out=h[:, oc, :], in0=po[:], scalar=modv[:, l, 40 + oc:41 + oc], in1=h[:, oc, :],
                                                                            op0=ALU.mult, op1=ALU.add),
                 reads=[pok, ("mod", l), ("h", par, oc)], writes=[("h", par, oc)])

    out_tokens = []
    for t in range(ntiles):
        par = t % 2
        ts_ = slice(t * TT, (t + 1) * TT)
        S.dma("sp", lambda par=par, ts_=ts_: nc.sync.dma_start(out=hT[par][:], in_=xT[:, :, ts_]),
              writes=[("h", par, kc) for kc in range(8)], chan=("x", par))
        S.dma("sp", lambda ts_=ts_: nc.sync.dma_start(out=ropeS[:], in_=roped[:, :, ts_]), writes=["rope"], chan="rope")
        for l in range(L):
            if do_ffn1:
                norm_to_u(par, l, 0, 0)
                ffn(par, l, 0, 1)
            if do_mix:
                norm_to_u(par, l, 2, 3)
                mixer(par, l, t)
            if do_ffn2:
                norm_to_u(par, l, 3, 6)
                ffn(par, l, 1, 4)
        h = hT[par]
        fb = L * 128

        def out_fn(kc, rstd, rk, h=h, par=par):
            S.op("dve", lambda: nc.vector.scalar_tensor_tensor(out=h[:, kc, :], in0=h[:, kc, :], scalar=vecs[:, fb + kc:fb + kc + 1], in1=rstd[:],
                                                               op0=ALU.mult, op1=ALU.mult),
                 reads=[("h", par, kc), rk, "vecs"], writes=[("h", par, kc)])
        norm_mod(h, lambda kc, par=par: ("h", par, kc), None, None, None, out_fn)
        tok = S.dma("sp", lambda par=par, ts_=ts_: nc.sync.dma_start(out=oT[:, :, ts_], in_=hT[par][:]),
                    reads=[("h", par, kc) for kc in range(8)], chan=("o", par))
        out_tokens.append(tok)
    S.emit(final_tokens=out_tokens[-2:])
    return nc, S


def prepare_inputs(inputs, seq=SEQ, depth=DEPTH, ncore=NCORE):
    f = lambda k: np.asarray(inputs[k], dtype=np.float32)
    x = f("x")
    c = f("c")
    cF, cB, rope, _ = _const_tables(seq)
    L = depth
    w_ada = np.concatenate([_ada_units(f("ada_w")[l]) for l in range(L)], axis=0)
    w_f13 = np.concatenate([_ffn13_units(f(a)[l], f(b)[l]) for l in range(L) for (a, b) in (("ffn1_w1", "ffn1_w3"), ("ffn2_w1", "ffn2_w3"))], axis=0)
    w_f2 = np.concatenate([_ffn2_units(f(a)[l]) for l in range(L) for a in ("ffn1_w2", "ffn2_w2")], axis=0)
    w_in = np.concatenate([_layer_units_in(f("w_in")[l]) for l in range(L)], axis=0)
    w_p = np.concatenate([np.concatenate([_sq_units(f("p_ret")[l]), _sq_units(f("p_conv")[l]), _sq_units(f("p_swa")[l], _SWA_ROWPERM),
                                          _sq_units(f("w_out")[l])], axis=0) for l in range(L)], axis=0)
    vecs = np.zeros((128, L * 128 + 8), np.float32)
    for l in range(L):
        vb = l * 128
        vecs[:, vb:vb + 72] = _fm(f("ada_b")[l])
        vecs[:, vb + 72:vb + 80] = _fm(f("ffn1_norm")[l])
        vecs[:, vb + 80:vb + 88] = _fm(f("mix_norm")[l])
        vecs[:, vb + 88:vb + 96] = _fm(f("ffn2_norm")[l])
        cw = f("conv_w")[l]
        for j in range(3):
            vecs[:, vb + 96 + j * 8:vb + 104 + j * 8] = _fm(cw[j])
        sk = f("swa_sinks")[l]
        for cc in range(8):
            vecs[0:64, vb + 120 + cc] = sk[cc]
            vecs[64:128, vb + 120 + cc] = sk[cc + 8]
    vecs[:, L * 128:L * 128 + 8] = _fm(f("final_norm"))
    shared = dict(vecs=vecs, cF=cF, cB=cB, rope=rope, w_ada=w_ada, w_f13=w_f13, w_f2=w_f2, w_in=w_in, w_p=w_p)
    in_maps = []
    for b in range(ncore):
        m = dict(shared)
        m["xT"] = np.ascontiguousarray(x[b, :seq].T.reshape(8, 128, seq).transpose(1, 0, 2))
        m["cT"] = _fm(c[b])
        in_maps.append(m)
    return in_maps


def kernel(**inputs):
    in_maps = prepare_inputs(inputs)
    nc, _ = build_program()
    res = run_bass_kernel_spmd(nc, in_maps, core_ids=list(range(NCORE)))
    out = np.empty((NCORE, SEQ, D), np.float32)
    for b in range(NCORE):
        o = np.asarray(res.results[b]["oT"])
        out[b] = o.transpose(2, 1, 0).reshape(SEQ, D)
    return out
```
